# Optimizing a Trainium2 kernel written in Bass

```python
import math
import jax, jax.numpy as jnp
from jax import lax
import numpy as np

D_MODEL = 1024
BATCH = 8
SEQ = 4096
DEPTH = 1

D_MIX = D_MODEL
ATTN_HEADS = 8
ATTN_HEAD_DIM = 64
ATTN_WIDTH = ATTN_HEADS * ATTN_HEAD_DIM
MLSTM_HEADS = 4
MLSTM_HEAD_DIM = 128
MLSTM_WIDTH = MLSTM_HEADS * MLSTM_HEAD_DIM
DILATED_PATTERNS = ((128, 1), (512, 4), (2048, 16))
ATTN_BLOCK = 128
ROPE_THETA = 500000.0
ROPE_DIM = ATTN_HEAD_DIM // 4
CONV_WIDTH = 4
MLSTM_CHUNK = 64
D_FF = 4 * D_MODEL
PLE_DIM = 256
RMS_EPS = 1e-6
IN_PROJ_WIDTH = 3 * ATTN_WIDTH + 4 * MLSTM_WIDTH + 2 * MLSTM_HEADS

kernel_name = "hymba_dilated_attn_mlstm_block"


def rms_norm(x, g):
    xf = x.astype(jnp.float32)
    y = xf * lax.rsqrt(jnp.mean(xf * xf, axis=-1, keepdims=True) + RMS_EPS)
    return (y * g.astype(jnp.float32)).astype(x.dtype)


def partial_rope(x, pos):
    half = ROPE_DIM // 2
    inv_freq = jnp.power(ROPE_THETA, -jnp.arange(half, dtype=jnp.float32) / half)
    ang = pos.astype(jnp.float32)[:, None] * inv_freq[None, :]
    cos, sin = jnp.cos(ang), jnp.sin(ang)
    xr = x[..., :ROPE_DIM].astype(jnp.float32)
    x1, x2 = xr[..., :half], xr[..., half:]
    rot = jnp.concatenate([x1 * cos - x2 * sin, x2 * cos + x1 * sin], axis=-1)
    return jnp.concatenate([rot.astype(x.dtype), x[..., ROPE_DIM:]], axis=-1)


def dilated_window_partial(q, k, v, window, dilation):
    B, H, S, hd = q.shape
    n_back = window // dilation
    L = S // dilation
    nb = -(-L // ATTN_BLOCK)
    Lp = nb * ATTN_BLOCK
    blk = ATTN_BLOCK

    def to_sub(t):
        t = t.reshape(B, H, L, dilation, hd).transpose(0, 1, 3, 2, 4)
        t = jnp.pad(t, ((0, 0), (0, 0), (0, 0), (0, Lp - L), (0, 0)))
        return t.reshape(B, H, dilation, nb, blk, hd)

    def with_prev(t):
        prev = jnp.pad(t, ((0, 0), (0, 0), (0, 0), (1, 0), (0, 0), (0, 0)))[:, :, :, :-1]
        return jnp.concatenate([prev, t], axis=4)

    qs = to_sub(q)
    kb = with_prev(to_sub(k))
    vb = with_prev(to_sub(v))
    s = jnp.einsum('bhrnqd,bhrnkd->bhrnqk', qs, kb) * (1.0 / math.sqrt(hd))
    qi = jnp.arange(blk)[:, None]
    ki = jnp.arange(2 * blk)[None, :]
    dist = qi + blk - ki
    key_pos = jnp.arange(nb)[:, None, None] * blk + ki[None] - blk
    valid = (dist >= 0) & (dist <= n_back) & (key_pos >= 0)
    s = jnp.where(valid, s, -jnp.inf)
    m = jnp.max(s, axis=-1)
    pexp = jnp.exp(s - m[..., None])
    den = jnp.sum(pexp, axis=-1)
    num = jnp.einsum('bhrnqk,bhrnkd->bhrnqd', pexp, vb)

    def back_vec(t):
        t = t.reshape(B, H, dilation, Lp, hd)[:, :, :, :L]
        return t.transpose(0, 1, 3, 2, 4).reshape(B, H, S, hd)

    def back_scalar(t):
        t = t.reshape(B, H, dilation, Lp)[:, :, :, :L]
        return t.transpose(0, 1, 3, 2).reshape(B, H, S)

    return back_vec(num), back_scalar(m), back_scalar(den)


def dilated_attention(q, k, v):
    parts = [dilated_window_partial(q, k, v, w, d) for (w, d) in DILATED_PATTERNS]
    m_all = jnp.max(jnp.stack([pm for (_, pm, _) in parts]), axis=0)
    num = sum(pn * jnp.exp(pm - m_all)[..., None] for (pn, pm, _) in parts)
    den = sum(pd * jnp.exp(pm - m_all) for (_, pm, pd) in parts)
    return num / den[..., None]


def mlstm_chunkwise(q, k, v, i_pre, f_pre):
    B, H, S, dh = q.shape
    Lc = MLSTM_CHUNK
    nc = S // Lc
    k = k * (1.0 / math.sqrt(dh))
    logf = jax.nn.log_sigmoid(f_pre)

    def chunks(t):
        t = t.reshape(B, H, nc, Lc, *t.shape[3:])
        return jnp.moveaxis(t, 2, 0)

    xs = (chunks(q), chunks(k), chunks(v), chunks(i_pre), chunks(logf))
    causal = jnp.tril(jnp.ones((Lc, Lc), dtype=bool))

    def step(carry, inp):
        C, n, m = carry
        qc, kc, vc, ic, fc = inp
        b = jnp.cumsum(fc, axis=-1)
        log_d = b[..., :, None] - b[..., None, :] + ic[..., None, :]
        log_d = jnp.where(causal, log_d, -jnp.inf)
        log_inter = b + m[..., None]
        m_t = jnp.maximum(log_inter, jnp.max(log_d, axis=-1))
        w_intra = jnp.exp(log_d - m_t[..., None])
        w_inter = jnp.exp(log_inter - m_t)
        qk = jnp.einsum('bhtd,bhsd->bhts', qc, kc) * w_intra
        num = (w_inter[..., None] * jnp.einsum('bhtd,bhde->bhte', qc, C)
               + jnp.einsum('bhts,bhse->bhte', qk, vc))
        den = w_inter * jnp.einsum('bhtd,bhd->bht', qc, n) + jnp.sum(qk, axis=-1)
        h = num / jnp.maximum(jnp.abs(den), jnp.exp(-m_t))[..., None]
        b_last = b[..., -1]
        log_s = b_last[..., None] - b + ic
        m_new = jnp.maximum(b_last + m, jnp.max(log_s, axis=-1))
        decay = jnp.exp(b_last + m - m_new)
        ws = jnp.exp(log_s - m_new[..., None])
        C_new = decay[..., None, None] * C + jnp.einsum('bhs,bhsd,bhse->bhde', ws, kc, vc)
        n_new = decay[..., None] * n + jnp.einsum('bhs,bhsd->bhd', ws, kc)
        return (C_new, n_new, m_new), h

    init = (jnp.zeros((B, H, dh, dh), jnp.float32),
            jnp.zeros((B, H, dh), jnp.float32),
            jnp.zeros((B, H), jnp.float32))
    _, hs = lax.scan(step, init, xs)
    return jnp.moveaxis(hs, 0, 2).reshape(B, H, S, dh)


def causal_short_conv(x, w, b):
    S = x.shape[1]
    xp = jnp.pad(x, ((0, 0), (CONV_WIDTH - 1, 0), (0, 0)))
    return sum(w[j] * xp[:, j:j + S] for j in range(CONV_WIDTH)) + b


def split_heads(t, n_heads, hd):
    B, S, _ = t.shape
    return t.reshape(B, S, n_heads, hd).transpose(0, 2, 1, 3)


def merge_heads(t):
    B, H, S, hd = t.shape
    return t.transpose(0, 2, 1, 3).reshape(B, S, H * hd)


def setup_inputs(seed: int = 0) -> dict:
    key = jax.random.key(seed)
    ks = jax.random.split(key, 20)
    f32 = jnp.float32

    def nrm(k, shape, scale):
        return jax.random.normal(k, shape, f32) * scale

    x = jax.random.normal(ks[0], (BATCH, SEQ, D_MODEL), f32)
    p = jax.random.normal(ks[1], (DEPTH, BATCH, SEQ, PLE_DIM), f32)
    norm_mix_g = 1.0 + nrm(ks[2], (DEPTH, D_MODEL), 0.02)
    w_in = nrm(ks[3], (DEPTH, D_MODEL, IN_PROJ_WIDTH), D_MODEL ** -0.5)
    conv_w = nrm(ks[4], (DEPTH, CONV_WIDTH, 2 * MLSTM_WIDTH), CONV_WIDTH ** -0.5)
    conv_b = nrm(ks[5], (DEPTH, 2 * MLSTM_WIDTH), 0.01)
    i_bias = nrm(ks[6], (DEPTH, MLSTM_HEADS), 0.1)
    f_bias = (jnp.linspace(3.0, 6.0, MLSTM_HEADS, dtype=f32)[None, :]
              + nrm(ks[7], (DEPTH, MLSTM_HEADS), 0.1))
    gate_b = jnp.concatenate([i_bias, f_bias], axis=-1)
    mlstm_norm_g = 1.0 + nrm(ks[8], (DEPTH, MLSTM_WIDTH), 0.02)
    w_out = nrm(ks[9], (DEPTH, D_MIX, D_MODEL), D_MIX ** -0.5)
    norm_mlp_g = 1.0 + nrm(ks[10], (DEPTH, D_MODEL), 0.02)
    w_up = nrm(ks[11], (DEPTH, D_MODEL, D_FF), D_MODEL ** -0.5)
    w_down = nrm(ks[12], (DEPTH, D_FF, D_MODEL), D_FF ** -0.5)
    norm_ple_g = 1.0 + nrm(ks[13], (DEPTH, D_MODEL), 0.02)
    w_ple_gate = nrm(ks[14], (DEPTH, D_MODEL, D_MODEL), D_MODEL ** -0.5)
    w_ple = nrm(ks[15], (DEPTH, PLE_DIM, D_MODEL), PLE_DIM ** -0.5)
    final_norm_g = 1.0 + nrm(ks[16], (D_MODEL,), 0.02)
    return {"x": x, "p": p, "norm_mix_g": norm_mix_g, "w_in": w_in,
            "conv_w": conv_w, "conv_b": conv_b, "gate_b": gate_b,
            "mlstm_norm_g": mlstm_norm_g, "w_out": w_out, "norm_mlp_g": norm_mlp_g,
            "w_up": w_up, "w_down": w_down, "norm_ple_g": norm_ple_g,
            "w_ple_gate": w_ple_gate, "w_ple": w_ple, "final_norm_g": final_norm_g}


def reference(x, p, norm_mix_g, w_in, conv_w, conv_b, gate_b, mlstm_norm_g, w_out,
              norm_mlp_g, w_up, w_down, norm_ple_g, w_ple_gate, w_ple, final_norm_g):
    B, S, _ = x.shape
    pos = jnp.arange(S, dtype=jnp.int32)
    A, M = ATTN_WIDTH, MLSTM_WIDTH
    h = x
    for layer in range(DEPTH):
        u = rms_norm(h, norm_mix_g[layer])
        proj = u @ w_in[layer]
        aq = proj[..., 0:A]
        ak = proj[..., A:2 * A]
        av = proj[..., 2 * A:3 * A]
        o0 = 3 * A
        mqk = proj[..., o0:o0 + 2 * M]
        mv = proj[..., o0 + 2 * M:o0 + 3 * M]
        mo = proj[..., o0 + 3 * M:o0 + 4 * M]
        gates = (proj[..., o0 + 4 * M:] + gate_b[layer]).astype(jnp.float32)

        qa = partial_rope(split_heads(aq, ATTN_HEADS, ATTN_HEAD_DIM), pos).astype(jnp.float32)
        ka = partial_rope(split_heads(ak, ATTN_HEADS, ATTN_HEAD_DIM), pos).astype(jnp.float32)
        va = split_heads(av, ATTN_HEADS, ATTN_HEAD_DIM).astype(jnp.float32)
        attn_out = merge_heads(dilated_attention(qa, ka, va)).astype(h.dtype)

        qk_c = jax.nn.silu(causal_short_conv(mqk, conv_w[layer], conv_b[layer]))
        qm = split_heads(qk_c[..., :M], MLSTM_HEADS, MLSTM_HEAD_DIM).astype(jnp.float32)
        km = split_heads(qk_c[..., M:], MLSTM_HEADS, MLSTM_HEAD_DIM).astype(jnp.float32)
        vm = split_heads(mv, MLSTM_HEADS, MLSTM_HEAD_DIM).astype(jnp.float32)
        i_pre = gates[..., :MLSTM_HEADS].transpose(0, 2, 1)
        f_pre = gates[..., MLSTM_HEADS:].transpose(0, 2, 1)
        hm = mlstm_chunkwise(qm, km, vm, i_pre, f_pre)
        hm = hm * lax.rsqrt(jnp.mean(hm * hm, axis=-1, keepdims=True) + RMS_EPS)
        hm = hm * mlstm_norm_g[layer].astype(jnp.float32).reshape(MLSTM_HEADS, 1, MLSTM_HEAD_DIM)
        mlstm_out = (jax.nn.sigmoid(mo.astype(jnp.float32)) * merge_heads(hm)).astype(h.dtype)

        mix = jnp.concatenate([attn_out, mlstm_out], axis=-1)
        h = h + mix @ w_out[layer]

        u = rms_norm(h, norm_mlp_g[layer])
        h = h + jnp.square(jax.nn.relu(u @ w_up[layer])) @ w_down[layer]

        gate = jax.nn.sigmoid(rms_norm(h, norm_ple_g[layer]) @ w_ple_gate[layer])
        h = h + gate * (p[layer] @ w_ple[layer])
    return rms_norm(h, final_norm_g)
```

```python
import contextlib
import math
import numpy as np
import ml_dtypes
import concourse.bass as bass
import concourse.mybir as mybir
from concourse.bass_utils import run_bass_kernel_spmd

F32 = mybir.dt.float32
BF16 = mybir.dt.bfloat16
AF = mybir.ActivationFunctionType
ALU = mybir.AluOpType

SEQ = 4096
DM = 1024
NIN = 3592
EPS = 1e-6


class _Op:
    __slots__ = ("eng", "fn", "deps", "dma", "idx", "ms", "cnt", "sem", "prev_wait")

    def __init__(self, eng, fn, dma):
        self.eng = eng
        self.fn = fn
        self.dma = dma
        self.deps = set()
        self.ms = False
        self.cnt = 0
        self.sem = None
        self.prev_wait = 0


class Sched:
    ENGS = ("pe", "act", "dve", "pool", "sp")

    def __init__(self, nc, ndma_sems=12):
        self.nc = nc
        self.ops = []
        self.lastw = {}
        self.readers = {}
        self.ndma = ndma_sems
        self.last_on = {e: None for e in self.ENGS}
        self.dma_n = {"sp": 0, "pool": 0, "act": 0}
        self.last_dma = {}
        self.defer = None

    def _add(self, op, reads, writes):
        op.idx = len(self.ops)
        writes = list(writes) + [k for k in reads if isinstance(k, tuple) and k[0] == "ps" and k not in writes]
        deps = set()
        for k in reads:
            w = self.lastw.get(k)
            if w is not None:
                deps.add(w)
        for k in writes:
            w = self.lastw.get(k)
            if w is not None:
                deps.add(w)
            for r in self.readers.get(k, ()):
                deps.add(r)
        deps.discard(op.idx)
        for k in reads:
            self.readers.setdefault(k, []).append(op.idx)
        for k in writes:
            self.lastw[k] = op.idx
            self.readers[k] = []
        if op.eng == "pe" and not op.dma:
            deps = {d for d in deps if not (self.ops[d].eng == "pe" and not self.ops[d].dma)}
        op.deps = deps
        self.ops.append(op)
        if not op.dma:
            self.last_on[op.eng] = op.idx
        else:
            n = self.dma_n[op.eng]
            self.dma_n[op.eng] = n + 1
            self.last_dma[(op.eng, n % self.ndma)] = op.idx
        return op

    def add(self, eng, fn, reads=(), writes=()):
        if self.defer is not None:
            self.defer.append((eng, fn, False, list(reads), list(writes)))
            return None
        return self._add(_Op(eng, fn, False), reads, writes)

    def dma(self, eng, out, in_, reads=(), writes=(), **kw):
        def fn(e):
            return e.dma_start(out=out, in_=in_, **kw)
        if self.defer is not None:
            self.defer.append((eng, fn, True, list(reads), list(writes)))
            return None
        return self._add(_Op(eng, fn, True), reads, writes)

    def begin_defer(self):
        self.defer = []

    def end_defer(self):
        q, self.defer = self.defer, None
        return {"ops": q, "live": set()}

    def flush_group(self, q):
        ops, live = q["ops"], q["live"]
        while ops:
            eng, fn, dma, reads, writes = ops.pop(0)
            if eng == "pe" and not dma:
                live.update(k for k in writes if isinstance(k, tuple) and k[0] == "ps")
            else:
                live.difference_update(k for k in reads if isinstance(k, tuple) and k[0] == "ps")
            self._add(_Op(eng, fn, dma), reads, writes)
            if not live:
                break

    def flush_all(self, q):
        while q["ops"]:
            self.flush_group(q)

    def barrier(self):
        allprev = set()
        for e in self.ENGS:
            if self.last_on[e] is not None:
                allprev.add(self.last_on[e])
        allprev.update(self.last_dma.values())
        for e in self.ENGS:
            op = _Op(e, None, False)
            op.idx = len(self.ops)
            op.deps = set(allprev)
            self.ops.append(op)

    def emit(self):
        nc = self.nc
        ops = self.ops
        for o in ops:
            for d in o.deps:
                ops[d].ms = True
        with contextlib.ExitStack() as st:
            esem = {e: st.enter_context(nc.semaphore("s_" + e)) for e in self.ENGS}
            dsem = {e: [st.enter_context(nc.semaphore("d_%s%d" % (e, i))) for i in range(self.ndma)]
                    for e in ("sp", "pool", "act")}
            cnt = {e: 0 for e in self.ENGS}
            dn = {e: 0 for e in dsem}
            final_dma = {}
            for o in ops:
                if o.dma:
                    n = dn[o.eng]
                    dn[o.eng] += 1
                    j = n % self.ndma
                    o.sem = dsem[o.eng][j]
                    o.cnt = 16 * (n // self.ndma + 1)
                    o.prev_wait = 16 * (n // self.ndma)
                    final_dma[(o.eng, j)] = (o.sem, o.cnt)
                elif o.ms and o.fn is not None:
                    cnt[o.eng] += 1
                    o.cnt = cnt[o.eng]
                    o.sem = esem[o.eng]
            block = st.enter_context(nc.Block())

            def stream(ename, e):
                seen = {}

                def wait(sem, key, val):
                    if val <= 0 or seen.get(key, 0) >= val:
                        return
                    e.wait_ge(sem, val)
                    seen[key] = val

                for o in ops:
                    if o.eng != ename:
                        continue
                    for d in sorted(o.deps):
                        p = ops[d]
                        if p.fn is None:
                            continue
                        if p.dma:
                            wait(p.sem, id(p.sem), p.cnt)
                        else:
                            wait(p.sem, p.eng, p.cnt)
                    if o.fn is None:
                        continue
                    if o.dma:
                        wait(o.sem, id(o.sem), o.prev_wait)
                        o.fn(e).then_inc(o.sem, 16)
                    else:
                        ins = o.fn(e)
                        if o.ms:
                            ins.then_inc(o.sem, 1)
                if ename == "sp":
                    for (sem, c) in final_dma.values():
                        wait(sem, id(sem), c)

            @block.tensor
            def _(e):
                stream("pe", e)

            @block.scalar
            def _(e):
                stream("act", e)

            @block.vector
            def _(e):
                stream("dve", e)

            @block.gpsimd
            def _(e):
                stream("pool", e)

            @block.sync
            def _(e):
                stream("sp", e)


def build_nc(debug=False, phases=(1, 2, 3, 4)):
    nc = bass.Bass("TRN2", target_bir_lowering=False)
    S = Sched(nc)
    din = lambda n, sh, dt=F32: nc.dram_tensor(n, sh, dt, kind="ExternalInput").ap()
    x = din("x", [SEQ, DM])
    pin = din("p", [SEQ, 256])
    w_in = din("w_in", [DM, NIN])
    w_out = din("w_out", [DM, DM])
    w_up = din("w_up", [DM, 4096])
    w_down = din("w_down", [4096, DM])
    w_pg = din("w_pg", [DM, DM])
    w_ple = din("w_ple", [256, DM])
    gT_d = din("gT", [128, 24])
    gfin_d = din("gfin", [128, DM])
    gm_d = din("gm", [128, 512])
    cw_d = din("cw", [128, 32])
    cb_d = din("cb", [128, 8])
    gb_d = din("gb", [128, 2])
    identb_d = din("identb", [128, 128], BF16)
    identf_d = din("identf", [128, 128])
    amask_d = din("amask", [128, 256], BF16)
    amask2_d = din("amask2", [128, 256], BF16)
    mmask_d = din("mmask", [128, 128])
    rm_d = din("rm", [128, 128], BF16)
    ropec_d = din("ropec", [128, SEQ])
    ropes_d = din("ropes", [128, SEQ])
    lmat_d = din("lmat", [128, 128])
    out = nc.dram_tensor("out", [SEQ, DM], F32, kind="ExternalOutput").ap()
    sk = "ExternalOutput" if debug else "Internal"
    xT_d = nc.dram_tensor("xT_d", [DM, SEQ], BF16, kind=sk).ap()
    mixT_d = nc.dram_tensor("mixT_d", [DM, SEQ], BF16, kind=sk).ap()
    gsc = nc.dram_tensor("gsc", [8, SEQ], F32, kind=sk).ap()
    xT_v = xT_d.rearrange("(kc p) t -> p kc t", p=128)
    mixT_v = mixT_d.rearrange("(kc p) t -> p kc t", p=128)
    w_in_v = w_in.rearrange("(kc p) n -> p kc n", p=128)

    with contextlib.ExitStack() as st:
        sb = lambda n, sh, dt=F32: st.enter_context(nc.sbuf_tensor("sb_" + n, sh, dt))
        ps = [st.enter_context(nc.psum_tensor("ps%d" % i, [128, 512], F32)) for i in range(8)]
        psb = [t[:].bitcast(BF16) for t in ps]

        identb = sb("identb", [128, 128], BF16)
        identf = sb("identf", [128, 128])
        amask = sb("amask", [128, 256], BF16)
        amask2 = sb("amask2", [128, 256], BF16)
        mmask = sb("mmask", [128, 128])
        rm = sb("rm", [128, 128], BF16)
        lmat = sb("lmat", [128, 128])
        gT = sb("gT", [128, 24])
        gfin = sb("gfin", [128, DM])
        gm = sb("gm", [128, 512])
        cw = sb("cw", [128, 32])
        cb = sb("cb", [128, 8])
        gb = sb("gb", [128, 2])
        onesf = sb("onesf", [128, 128])
        junk = sb("junk", [128, DM], BF16)
        dummy = sb("dummy", [128, 2])
        ckeys = []
        for t_, d_ in ((identb, identb_d), (identf, identf_d), (amask, amask_d), (amask2, amask2_d), (mmask, mmask_d), (rm, rm_d),
                       (lmat, lmat_d), (gT, gT_d), (gfin, gfin_d), (gm, gm_d), (cw, cw_d), (cb, cb_d), (gb, gb_d)):
            S.dma("sp", t_[:], d_, writes=[("c", t_.name)])
            ckeys.append(("c", t_.name))
        S.add("pool", lambda e: e.memset(onesf[:], 1.0), writes=[("c", "ones")])
        S.add("pool", lambda e: e.memset(dummy[:], 0.0), reads=ckeys + [("c", "ones")], writes=["const"])

        def mm_group(out_ap, pairs, reads, writes, **kw):
            def fn(e):
                n = len(pairs)
                ins = None
                for i, (l, r) in enumerate(pairs):
                    ins = e.matmul(out_ap, lhsT=l, rhs=r, start=(i == 0), stop=(i == n - 1), **kw)
                return ins
            S.add("pe", fn, reads=list(reads) + ["const"], writes=writes)

        cnt = {"xin": 0}
        xin = [None, None]

        def load_xin(tt):
            b = cnt["xin"] % 2
            cnt["xin"] += 1
            S.dma("sp", xin[b][:], xT_v[:, :, tt * 512:(tt + 1) * 512], reads=[("xT_d", tt)], writes=[("xin", b)])
            return b

        def rms_rstd(src_ap, col_ss, col_sq, col_r, stat, rkeys, tag, n=DM):
            S.add("act", lambda e: e.activation(out=junk[:, 0:n], in_=src_ap, func=AF.Square,
                                                accum_out=stat[:, col_ss:col_ss + 1]),
                  reads=rkeys, writes=[(tag, "ss0")])
            S.add("act", lambda e: e.activation(out=stat[:, col_sq:col_sq + 1], in_=stat[:, col_ss:col_ss + 1], func=AF.Copy),
                  reads=[(tag, "ss0")], writes=[(tag, "ss")])
            S.add("act", lambda e: e.activation(out=stat[:, col_sq:col_sq + 1], in_=stat[:, col_ss:col_ss + 1],
                                                func=AF.Sqrt, scale=1.0 / n, bias=EPS),
                  reads=[(tag, "ss")], writes=[(tag, "sq")])
            S.add("dve", lambda e: e.reciprocal(out=stat[:, col_r:col_r + 1], in_=stat[:, col_sq:col_sq + 1]),
                  reads=[(tag, "sq")], writes=[(tag, "r")])

        stg = contextlib.ExitStack()
        sbg = lambda n, sh, dt=F32: stg.enter_context(nc.sbuf_tensor("sb_" + n, sh, dt))
        Wg = sbg("Wg", [128, 8, 8], BF16)
        gst = [sbg("gst%d" % i, [128, 2, 512]) for i in range(2)]
        G = {n_: sbg("G_" + n_, [128, 128]) for n_ in
             ("gI", "gF", "ef", "sp", "nFl", "nF", "gi", "a", "Ml", "e", "e2", "eps")}
        col = sbg("col", [128, 8])
        rows = sbg("rows", [128, 5, 128])
        dec_bc = sbg("dec_bc", [128, 128])
        etok = sbg("etok", [128, 3, 128])

        def gates_prologue():
            S.dma("pool", Wg[:], w_in_v[:, :, 3584:3592], writes=["Wg"])
            for tt in range(8):
                xb = load_xin(tt)
                tsl = slice(tt * 512, (tt + 1) * 512)
                gb_ = tt % 2
                for gi_ in range(2):
                    bk = 6 + gi_
                    mm_group(ps[bk][0:4, :], [(Wg[:, kc, gi_ * 4:gi_ * 4 + 4], xin[xb][:, kc, :]) for kc in range(8)],
                             ["Wg", ("xin", xb)], [("ps", bk)])
                    S.add("act", lambda e, gi_=gi_, bk=bk, gb_=gb_: e.activation(
                        out=gst[gb_][0:4, gi_, :], in_=ps[bk][0:4, :], func=AF.Copy),
                        reads=[("ps", bk)], writes=[("gst", gb_, gi_)])
                S.dma("sp", gsc.rearrange("(g h) t -> h g t", g=2)[:, :, tsl], gst[gb_][0:4, :, :],
                      reads=[("gst", gb_, 0), ("gst", gb_, 1)], writes=[("gsc", tt)])
            S.dma("sp", G["gI"][:], gsc[0:4, :].rearrange("h (c t) -> (h c) t", t=128), reads=[("gsc", t_) for t_ in range(8)], writes=["gI"])
            S.dma("sp", G["gF"][:], gsc[4:8, :].rearrange("h (c t) -> (h c) t", t=128), reads=[("gsc", t_) for t_ in range(8)], writes=["gF"])
            S.add("dve", lambda e: e.tensor_scalar(out=col[:, 0:2], in0=gb[:, 0:2], scalar1=-1.0, scalar2=None, op0=ALU.mult),
                  reads=["const"], writes=["ngb"])
            S.add("act", lambda e: e.activation(out=G["ef"][:], in_=G["gF"][:], func=AF.Exp, scale=-1.0, bias=col[:, 1:2]),
                  reads=["gF", "ngb"], writes=["ef"])
            S.add("act", lambda e: e.activation(out=G["sp"][:], in_=G["ef"][:], func=AF.Ln, scale=1.0, bias=1.0),
                  reads=["ef"], writes=["sp"])
            S.add("dve", lambda e: e.tensor_tensor_scan(out=G["nFl"][:], data0=onesf[:, 0:128], data1=G["sp"][:],
                                                        initial=0.0, op0=ALU.mult, op1=ALU.add),
                  reads=["sp", "const"], writes=["nFl"])
            mm_group(ps[6][:, 0:1], [(lmat[:], G["nFl"][:, 127:128])], ["nFl"], [("ps", 6)])
            S.add("act", lambda e: e.activation(out=col[:, 2:3], in_=ps[6][:, 0:1], func=AF.Copy),
                  reads=[("ps", 6)], writes=["carr"])
            S.add("dve", lambda e: e.tensor_scalar(out=G["nF"][:], in0=G["nFl"][:], scalar1=col[:, 2:3], scalar2=None, op0=ALU.add),
                  reads=["nFl", "carr"], writes=["nF"])
            S.add("act", lambda e: e.activation(out=G["gi"][:], in_=G["gI"][:], func=AF.Identity, bias=gb[:, 0:1]),
                  reads=["gI", "const"], writes=["gi"])
            S.add("dve", lambda e: e.tensor_tensor(out=G["a"][:], in0=G["gi"][:], in1=G["nF"][:], op=ALU.add),
                  reads=["gi", "nF"], writes=["a"])
            S.add("dve", lambda e: e.tensor_tensor_scan(out=G["Ml"][:], data0=G["a"][:], data1=G["a"][:],
                                                        initial=-1e30, op0=ALU.max, op1=ALU.max),
                  reads=["a"], writes=["Ml"])
            S.add("pe", lambda e: e.transpose(out=ps[7][0:1, 0:128], in_=G["Ml"][:, 127:128], identity=identf[:]),
                  reads=["Ml", "const"], writes=[("ps", 7)])
            S.add("act", lambda e: e.activation(out=rows[0:1, 0, :], in_=ps[7][0:1, 0:128], func=AF.Copy),
                  reads=[("ps", 7)], writes=["trow"])
            for h in range(4):
                S.add("dve", lambda e, h=h: e.tensor_tensor_scan(
                    out=rows[0:1, 1, h * 32:(h + 1) * 32], data0=rows[0:1, 0, h * 32:(h + 1) * 32],
                    data1=rows[0:1, 0, h * 32:(h + 1) * 32], initial=0.0, op0=ALU.max, op1=ALU.max),
                    reads=["trow"], writes=[("Irow", h)])
            S.add("pool", lambda e: e.memset(rows[0:1, 2, :], 0.0), writes=["Rrow"])
            S.add("dve", lambda e: e.tensor_copy(out=rows[0:1, 2, :].rearrange("p (h c) -> p h c", h=4)[:, :, 1:32],
                                                 in_=rows[0:1, 1, :].rearrange("p (h c) -> p h c", h=4)[:, :, 0:31]),
                  reads=[("Irow", h) for h in range(4)] + ["Rrow"], writes=["Rrow"])
            S.add("dve", lambda e: e.tensor_tensor(out=rows[0:1, 3, :], in0=rows[0:1, 2, :], in1=rows[0:1, 1, :], op=ALU.subtract),
                  reads=["Rrow"] + [("Irow", h) for h in range(4)], writes=["tmprow"])
            S.add("act", lambda e: e.activation(out=rows[0:1, 4, :], in_=rows[0:1, 3, :], func=AF.Exp),
                  reads=["tmprow"], writes=["drow"])

            def colmm(e):
                e.matmul(ps[6][:, 0:1], lhsT=rows[0:1, 2, :], rhs=onesf[0:1, 0:1], start=True, stop=True)
                return e.matmul(ps[6][:, 1:2], lhsT=rows[0:1, 1, :], rhs=onesf[0:1, 0:1], start=True, stop=True)
            S.add("pe", colmm, reads=["Rrow", "const"] + [("Irow", h) for h in range(4)], writes=[("ps", 6)])
            S.add("dve", lambda e: e.tensor_scalar(out=col[:, 4:6], in0=ps[6][:, 0:2], scalar1=-1.0, scalar2=None, op0=ALU.mult),
                  reads=[("ps", 6)], writes=["nrho"])
            mm_group(ps[7][:, 0:128], [(onesf[0:1, 0:128], rows[0:1, 4, :])], ["drow"], [("ps", 7)])
            S.add("act", lambda e: e.activation(out=dec_bc[:], in_=ps[7][:, 0:128], func=AF.Copy),
                  reads=[("ps", 7)], writes=["dec_bc"])
            S.add("act", lambda e: e.activation(out=G["e"][:], in_=G["a"][:], func=AF.Exp, bias=col[:, 4:5]),
                  reads=["a", "nrho"], writes=["e"])
            S.add("act", lambda e: e.activation(out=G["e2"][:], in_=G["a"][:], func=AF.Exp, bias=col[:, 5:6]),
                  reads=["a", "nrho"], writes=["e2"])
            S.add("act", lambda e: e.activation(out=G["eps"][:], in_=G["nF"][:], func=AF.Exp, bias=col[:, 4:5]),
                  reads=["nF", "nrho"], writes=["epsm"])

            def tr3(e):
                e.transpose(out=ps[6][:, 0:128], in_=G["e"][:], identity=identf[:])
                e.transpose(out=ps[6][:, 128:256], in_=G["e2"][:], identity=identf[:])
                return e.transpose(out=ps[6][:, 256:384], in_=G["eps"][:], identity=identf[:])
            S.add("pe", tr3, reads=["e", "e2", "epsm", "const", "nrho"], writes=[("ps", 6)])
            S.add("act", lambda e: e.activation(out=etok[:].rearrange("p a b -> p (a b)"), in_=ps[6][:, 0:384], func=AF.Copy),
                  reads=[("ps", 6)], writes=["etok"])


        with contextlib.ExitStack() as st1:
            sb1 = lambda n, sh, dt=F32: st1.enter_context(nc.sbuf_tensor("sb_" + n, sh, dt))
            xt = [sb1("xt%d" % i, [128, DM]) for i in range(6)]
            ub = [sb1("ub%d" % i, [128, DM], BF16) for i in range(2)]
            xTs = [sb1("xTs%d" % i, [128, 8, 512], BF16) for i in range(2)]
            st1t = sb1("st1t", [128, 96])
            if 1 in phases:
                def p1a_load(i):
                    b = i % 6
                    S.dma("sp", xt[b][:], x[i * 128:(i + 1) * 128, :], writes=[("xt", b)])
                    rms_rstd(xt[b][:], i, 32 + i, 64 + i, st1t, [("xt", b)], ("n1", i))

                def p1a_rest(i):
                    b = i % 6
                    u = i % 2
                    tt, sub = i // 4, i % 4
                    tb = tt % 2
                    S.add("dve", lambda e: e.tensor_scalar(out=ub[u][:], in0=xt[b][:], scalar1=st1t[:, 64 + i:65 + i],
                                                           scalar2=None, op0=ALU.mult),
                          reads=[("xt", b), (("n1", i), "r")], writes=[("ub", u)])

                    def tr(e):
                        ins = None
                        for kc in range(8):
                            ins = e.transpose(out=psb[u][:, kc * 128:(kc + 1) * 128],
                                              in_=ub[u][:, kc * 128:(kc + 1) * 128], identity=identb[:])
                        return ins
                    S.add("pe", tr, reads=[("ub", u), "const"], writes=[("ps", u)])
                    S.add("dve", lambda e: e.tensor_tensor(
                        out=xTs[tb][:, :, sub * 128:(sub + 1) * 128],
                        in0=psb[u].rearrange("p (c t) -> p c t", c=8),
                        in1=gT[:, 0:8].unsqueeze(2).to_broadcast([128, 8, 128]), op=ALU.mult),
                        reads=[("ps", u), "const"], writes=[("xTs", tb, sub)])
                    if sub == 3:
                        S.dma("pool", xT_v[:, :, tt * 512:(tt + 1) * 512], xTs[tb][:],
                              reads=[("xTs", tb, s_) for s_ in range(4)], writes=[("xT_d", tt)])
                for i in range(5):
                    p1a_load(i)
                for i in range(32):
                    p1a_rest(i)
                    if i + 5 < 32:
                        p1a_load(i + 5)
        S.barrier()

        with contextlib.ExitStack() as st2:
            sb2 = lambda n, sh, dt=F32: st2.enter_context(nc.sbuf_tensor("sb_" + n, sh, dt))
            if 2 in phases:
                xin[0] = sb2("xin0b", [128, 8, 512], BF16)
                xin[1] = sb2("xin1b", [128, 8, 512], BF16)
                Wa = [[sb2("Wa%d_%d" % (i, w), [128, 8, 128], BF16) for w in range(3)] for i in range(2)]
                qT = sb2("qT", [128, SEQ], BF16)
                kTh = [sb2("kTA", [128, SEQ], BF16), sb2("kTB", [128, SEQ], BF16)]
                vT = sb2("vT", [128, SEQ], BF16)
                Vb = [sb2("Vb%d" % i, [128, 32, 2, 65], BF16) for i in range(3)]
                acc = [sb2("acc%d" % i, [128, SEQ]) for i in range(2)]
                tabs = [sb2("tabs%d" % i, [128, 2, 512]) for i in range(2)]
                qraw = [sb2("qraw%d" % i, [128, 512], BF16) for i in range(2)]
                t1 = [sb2("t1_%d" % i, [128, 512]) for i in range(2)]
                t2 = [sb2("t2_%d" % i, [128, 512]) for i in range(2)]
                NPT = 6
                Eb = [sb2("Eb%d" % i, [128, 256], BF16) for i in range(NPT)]
                PT = [sb2("PT%d" % i, [128, 256], BF16) for i in range(NPT)]
                rdt = sb2("rdt", [128, SEQ])
                onb = [sb2("onb%d" % i, [128, 512], BF16) for i in range(2)]
                S.add("pool", lambda e: e.memset(kTh[0][64:128, :], 0.0), writes=["kz"])
                S.add("pool", lambda e: e.memset(kTh[1][0:64, :], 0.0), writes=["kz"])
                for i in range(3):
                    S.add("pool", lambda e, i=i: e.memset(Vb[i][:], 1.0), writes=[("Vb", i, g) for g in range(4)])
                rc = {"r": 0, "s": 0, "o": 0, "e": 0, "n": 0}
                S.begin_defer()
                gates_prologue()
                gq = S.end_defer()
                for j in range(4):
                    jb = j % 2
                    for w in range(3):
                        c0 = w * 512 + j * 128
                        S.dma("pool", Wa[jb][w][:], w_in_v[:, :, c0:c0 + 128], writes=[("Wa", jb, w)])
                    for tt in range(8):
                        xb = load_xin(tt)
                        tb = tt % 2
                        tsl = slice(tt * 512, (tt + 1) * 512)
                        S.dma("sp", tabs[tb][:, 0, :], ropec_d[:, tsl], writes=[("tabs", tb, 0)])
                        S.dma("sp", tabs[tb][:, 1, :], ropes_d[:, tsl], writes=[("tabs", tb, 1)])
                        pbk = (5, 6) if tb == 0 else (0, 1)
                        rbk = (7, 3)
                        vbk = 2 if tb == 0 else 4
                        for w in range(2):
                            bk = pbk[w]
                            mm_group(ps[bk][:, :], [(Wa[jb][w][:, kc, :], xin[xb][:, kc, :]) for kc in range(8)],
                                     [("Wa", jb, w), ("xin", xb)], [("ps", bk)])
                            S.add("act", lambda e, w=w, bk=bk: e.activation(out=qraw[w][:], in_=ps[bk][:, :], func=AF.Copy),
                                  reads=[("ps", bk)], writes=[("qraw", w)])
                        mm_group(ps[vbk][:, :], [(Wa[jb][2][:, kc, :], xin[xb][:, kc, :]) for kc in range(8)],
                                 [("Wa", jb, 2), ("xin", xb)], [("ps", vbk)])
                        S.add("act", lambda e, vbk=vbk, tsl=tsl: e.activation(out=vT[:, tsl], in_=ps[vbk][:, :], func=AF.Copy),
                              reads=[("ps", vbk)], writes=[("vT", tt)])
                        for w in range(2):
                            bk = pbk[w]
                            mm_group(ps[rbk[w]][:, :], [(rm[:], qraw[w][:])], [("qraw", w)], [("ps", rbk[w])])
                            S.add("dve", lambda e, w=w, bk=bk, tb=tb: e.tensor_tensor(
                                out=t1[w][:], in0=ps[bk][:, :], in1=tabs[tb][:, 0, :], op=ALU.mult),
                                reads=[("ps", bk), ("tabs", tb, 0)], writes=[("t1", w)])
                            S.add("dve", lambda e, w=w, tb=tb: e.tensor_tensor(
                                out=t2[w][:], in0=ps[rbk[w]][:, :], in1=tabs[tb][:, 1, :], op=ALU.mult),
                                reads=[("ps", rbk[w]), ("tabs", tb, 1)], writes=[("t2", w)])
                            if w == 0:
                                S.add("pool", lambda e, tsl=tsl: e.tensor_tensor(
                                    out=qT[:, tsl], in0=t1[0][:], in1=t2[0][:], op=ALU.add),
                                    reads=[("t1", 0), ("t2", 0)], writes=[("qT", tt)])
                            else:
                                S.add("pool", lambda e, tsl=tsl: e.tensor_tensor(
                                    out=kTh[0][0:64, tsl], in0=t1[1][0:64, :], in1=t2[1][0:64, :], op=ALU.add),
                                    reads=[("t1", 1), ("t2", 1)], writes=[("kT", 0, tt)])
                                S.add("pool", lambda e, tsl=tsl: e.tensor_tensor(
                                    out=kTh[1][64:128, tsl], in0=t1[1][64:128, :], in1=t2[1][64:128, :], op=ALU.add),
                                    reads=[("t1", 1), ("t2", 1)], writes=[("kT", 1, tt)])
                    pats = ((0, 1), (1, 4), (2, 16))
                    import os as _os
                    _stg = int(_os.environ.get('ATT_STAGE', '9'))
                    if _stg < 2:
                        continue
                    for (pi, d) in pats:
                        nb = 32 // d
                        for g8 in range(4):
                            vbk_ = 6 + (g8 % 2)

                            def trv(e, pi=pi, d=d, nb=nb, g8=g8, vbk_=vbk_):
                                ins = None
                                for q in range(8):
                                    bi = g8 * 8 + q
                                    r_, n_ = bi // nb, bi % nb
                                    base = d * 128 * n_ + r_
                                    ins = e.transpose(out=psb[vbk_][:, q * 128:(q + 1) * 128],
                                                      in_=vT[:, base:base + 127 * d + 1:d], identity=identb[:])
                                return ins
                            S.add("pe", trv, reads=[("vT", t_) for t_ in range(8)] + ["const"], writes=[("ps", vbk_)])
                            S.add("act" if g8 % 2 == 0 else "dve", (lambda e, pi=pi, g8=g8, vbk_=vbk_: e.activation(
                                out=Vb[pi][:, g8 * 8:(g8 + 1) * 8, :, 0:64],
                                in_=psb[vbk_].rearrange("p (q h c) -> p q h c", q=8, h=2), func=AF.Copy)) if g8 % 2 == 0 else
                                (lambda e, pi=pi, g8=g8, vbk_=vbk_: e.tensor_copy(
                                    out=Vb[pi][:, g8 * 8:(g8 + 1) * 8, :, 0:64],
                                    in_=psb[vbk_].rearrange("p (q h c) -> p q h c", q=8, h=2))),
                                reads=[("ps", vbk_)], writes=[("Vb", pi, g8)])
                    if _stg < 3:
                        continue
                    jobs = []
                    for (pi, d) in pats[:_stg - 2]:
                        nb = 32 // d
                        for hh in range(2):
                            for gi in range(32):
                                r_, k_ = gi // nb, gi % nb
                                jobs.append(dict(pi=pi, d=d, nb=nb, hh=hh, gi=gi, r_=r_, k_=k_, has_next=(k_ + 1 < nb)))
                    gbank = {}

                    def bank_of(pi, hh, grp):
                        key = (pi, hh, grp)
                        if key not in gbank:
                            gbank[key] = [4 + rc["o"] % 2, False]
                            rc["o"] += 1
                        return gbank[key]

                    def stage_a(jb_):
                        d, hh, r_, k_ = jb_["d"], jb_["hh"], jb_["r_"], jb_["k_"]
                        base = d * 128 * k_ + r_
                        nq_ = 256 if jb_["has_next"] else 128
                        ksl = slice(base, base + 127 * d + 1, d)
                        qsl = slice(base, base + (nq_ - 1) * d + 1, d)
                        lo_t, hi_t = base // 512, (base + (nq_ - 1) * d) // 512
                        sb_ = rc["s"] % 4
                        rc["s"] += 1
                        S.add("pe", lambda e: e.matmul(ps[sb_][:, 0:nq_], lhsT=kTh[hh][:, ksl], rhs=qT[:, qsl], start=True, stop=True),
                              reads=[("qT", t_) for t_ in range(lo_t, hi_t + 1)] +
                              [("kT", hh, t_) for t_ in range(lo_t, hi_t + 1)] + ["kz"], writes=[("ps", sb_)])
                        eb = rc["e"] % NPT
                        rc["e"] += 1
                        jb_["eb"] = eb
                        S.add("act", lambda e: e.activation(
                            out=Eb[eb][:, 0:nq_], in_=ps[sb_][:, 0:nq_], func=AF.Exp, scale=0.125),
                            reads=[("ps", sb_)], writes=[("Eb", eb)])
                        meng = "pool" if (rc["e"] % 2 == 0) else "dve"
                        S.add(meng, lambda e: e.tensor_tensor(
                            out=PT[eb][:, 0:nq_], in0=Eb[eb][:, 0:nq_], in1=amask2[:, 0:nq_], op=ALU.mult),
                            reads=[("Eb", eb), "const"], writes=[("PT", eb)])

                    def evac(pi, hh, grp, pob):
                        if pi == 0:
                            S.add("dve", lambda e: e.tensor_copy(
                                out=acc[hh][0:65, grp * 512:(grp + 1) * 512], in_=ps[pob][0:65, :]),
                                reads=[("ps", pob)], writes=[("acc", hh, grp)])
                        elif pi == 1:
                            r_, h2 = grp // 2, grp % 2

                            def ad(e):
                                av = acc[hh][0:65, h2 * 2048:(h2 + 1) * 2048].rearrange("p (s j r) -> p r s j", s=4, r=4)[:, r_]
                                return e.tensor_tensor(out=av, in0=ps[pob][0:65, :].rearrange("p (s j) -> p s j", s=4),
                                                       in1=av, op=ALU.add)
                            ks = [("acc", hh, h2 * 4 + q_) for q_ in range(4)]
                            S.add("dve", ad, reads=[("ps", pob)] + ks, writes=ks)
                        else:
                            def ad3(e):
                                av = acc[hh][0:65, :].rearrange("p (k j r) -> p r k j", k=2, r=16)[:, 2 * grp:2 * grp + 2]
                                return e.tensor_tensor(out=av, in0=ps[pob][0:65, :].rearrange("p (r k j) -> p r k j", r=2, k=2),
                                                       in1=av, op=ALU.add)
                            ks = [("acc", hh, q_) for q_ in range(8)]
                            S.add("dve", ad3, reads=[("ps", pob)] + ks, writes=ks)

                    def stage_b(jb_):
                        pi, hh, gi, eb, has_next = jb_["pi"], jb_["hh"], jb_["gi"], jb_["eb"], jb_["has_next"]
                        grp, slot = gi // 4, gi % 4
                        bk = bank_of(pi, hh, grp)
                        pob = bk[0]
                        vb_ = Vb[pi][:, gi, hh, :]
                        rk = [("PT", eb), ("Vb", pi, gi // 8)]
                        if has_next and slot < 3:
                            first = not bk[1]
                            bk[1] = True
                            S.add("pe", lambda e: e.matmul(ps[pob][0:65, slot * 128:(slot + 2) * 128], lhsT=vb_, rhs=PT[eb][:, 0:256],
                                                           start=first, stop=False, skip_group_check=True),
                                  reads=rk, writes=[("ps", pob)])
                            return
                        first = not bk[1]
                        bk[1] = True
                        S.add("pe", lambda e: e.matmul(ps[pob][0:65, slot * 128:(slot + 1) * 128], lhsT=vb_, rhs=PT[eb][:, 0:128],
                                                       start=first, stop=True, skip_group_check=True),
                              reads=rk, writes=[("ps", pob)])
                        if slot == 3:
                            evac(pi, hh, grp, pob)
                        if has_next:
                            bk2 = bank_of(pi, hh, grp + 1)
                            pob2 = bk2[0]
                            first2 = not bk2[1]
                            bk2[1] = True
                            S.add("pe", lambda e: e.matmul(ps[pob2][0:65, 0:128], lhsT=vb_, rhs=PT[eb][:, 128:256],
                                                           start=first2, stop=False, skip_group_check=True),
                                  reads=rk, writes=[("ps", pob2)])

                    LA = 3
                    for i_ in range(len(jobs) + LA):
                        if i_ < len(jobs):
                            stage_a(jobs[i_])
                        if i_ >= LA:
                            stage_b(jobs[i_ - LA])
                        if i_ % 8 == 0:
                            S.flush_group(gq)
                    if _stg < 6:
                        continue
                    for hh in range(2):
                        aks = [("acc", hh, g_) for g_ in range(8)]
                        S.add("act", lambda e, hh=hh: e.activation(out=rdt[64:65, :], in_=acc[hh][64:65, :], func=AF.Ln),
                              reads=aks, writes=["rdt"])
                        S.add("act", lambda e: e.activation(out=rdt[64:65, :], in_=rdt[64:65, :], func=AF.Exp, scale=-1.0),
                              reads=["rdt"], writes=["rdt"])
                        for grp in range(8):
                            gsl = slice(grp * 512, (grp + 1) * 512)
                            nb_ = rc["n"] % 2
                            rc["n"] += 1
                            bk = 6 + nb_
                            mm_group(ps[bk][0:64, :], [(onesf[64:65, 0:64], rdt[64:65, gsl])], ["rdt"], [("ps", bk)])
                            S.add("dve", lambda e, hh=hh, gsl=gsl, nb_=nb_, bk=bk: e.tensor_tensor(
                                out=onb[nb_][0:64, :], in0=acc[hh][0:64, gsl], in1=ps[bk][0:64, :], op=ALU.mult),
                                reads=[("ps", bk), ("acc", hh, grp)], writes=[("onb", nb_)])
                            row0 = (2 * j + hh) * 64
                            S.dma("sp", mixT_d[row0:row0 + 64, gsl], onb[nb_][0:64, :], reads=[("onb", nb_)],
                                  writes=[("mixT_d", grp)])
        if 2 in phases:
            S.flush_all(gq)
        S.barrier()

        with contextlib.ExitStack() as st3:
            sb3 = lambda n, sh, dt=F32: st3.enter_context(nc.sbuf_tensor("sb_" + n, sh, dt))
            if 3 in phases:
                xin[0] = sb3("xin0c", [128, 8, 512], BF16)
                xin[1] = sb3("xin1c", [128, 8, 512], BF16)
                if 2 not in phases:
                    gates_prologue()
                Wm = [[sb3("Wm%d_%d" % (i, w), [128, 8, 256 if w == 2 else 128], BF16) for w in range(3)] for i in range(2)]
                xc = [sb3("xc%d" % i, [128, SEQ + 3]) for i in range(2)]
                cacc = [sb3("cacc%d" % i, [128, SEQ]) for i in range(2)]
                qkm = [sb3("qTm", [128, SEQ], BF16), sb3("kTm", [128, SEQ], BF16)]
                ktok = sb3("ktok", [128, 32, 128], BF16)
                vpp = sb3("vpp", [128, 32, 129], BF16)
                vpq = sb3("vpq", [128, 32, 129], BF16)
                og = sb3("og", [128, 32, 128])
                sg = [sb3("sg%d" % i, [128, 128]) for i in range(2)]
                C32 = [sb3("C32_%d" % i, [128, 129]) for i in range(2)]
                Cb = [sb3("Cb%d" % i, [128, 129], BF16) for i in range(2)]
                PTm = [sb3("PTm%d" % i, [128, 128], BF16) for i in range(2)]
                om = [sb3("om%d" % i, [128, 128], BF16) for i in range(2)]
                omT = [sb3("omT%d" % i, [128, 512], BF16) for i in range(2)]
                stm = sb3("stm", [128, 32, 8])
                Ocp = [sb3("Ocp%d" % i, [128, 129]) for i in range(4)]
                S.add("pool", lambda e: e.memset(xc[0][:, 0:3], 0.0), writes=[("xcpad", 0)])
                S.add("pool", lambda e: e.memset(xc[1][:, 0:3], 0.0), writes=[("xcpad", 1)])
                inv_sqrt_dh = 1.0 / math.sqrt(128.0)

                def mlstm_head(h):
                    hb = h % 2
                    for w in range(4):
                        c0 = 1536 + w * 512 + h * 128
                        if w < 2:
                            S.dma("pool", Wm[hb][w][:], w_in_v[:, :, c0:c0 + 128], writes=[("Wm", hb, w)])
                        else:
                            S.dma("pool", Wm[hb][2][:, :, (w - 2) * 128:(w - 1) * 128], w_in_v[:, :, c0:c0 + 128], writes=[("Wm", hb, w)])

                    def conv_piece(w, pc):
                        ci = w * 4 + h
                        lo_, hi_ = pc * 1024, (pc + 1) * 1024
                        rk = [("xc", w, t_) for t_ in range(max(0, 2 * pc - 1), 2 * pc + 2)] + [("xcpad", w), "const"]
                        S.add("dve", lambda e: e.tensor_scalar(
                            out=cacc[w][:, lo_:hi_], in0=xc[w][:, lo_ + 3:hi_ + 3], scalar1=cw[:, ci * 4 + 3:ci * 4 + 4],
                            scalar2=None, op0=ALU.mult), reads=rk, writes=[("cacc", w, pc)])
                        for jt in (2, 1, 0):
                            S.add("dve", lambda e, jt=jt: e.scalar_tensor_tensor(
                                out=cacc[w][:, lo_:hi_], in0=xc[w][:, lo_ + jt:hi_ + jt], scalar=cw[:, ci * 4 + jt:ci * 4 + jt + 1],
                                in1=cacc[w][:, lo_:hi_], op0=ALU.mult, op1=ALU.add), reads=rk + [("cacc", w, pc)], writes=[("cacc", w, pc)])
                        S.add("act", lambda e: e.activation(
                            out=qkm[w][:, lo_:hi_], in_=cacc[w][:, lo_:hi_], func=AF.Silu, bias=cb[:, ci:ci + 1]),
                            reads=[("cacc", w, pc), "const"], writes=[("qkm", w, pc)])

                    def vo_sub(xb, tt, sub):
                        c_ = tt * 4 + sub
                        bk = 2 + (c_ % 2)
                        hc = h * 32 + c_

                        def vo(e):
                            ins = None
                            for kc in range(8):
                                ins = e.matmul(ps[bk][:, 0:256], lhsT=xin[xb][:, kc, sub * 128:(sub + 1) * 128],
                                               rhs=Wm[hb][2][:, kc, :], start=(kc == 0), stop=(kc == 7))
                            return ins
                        S.add("pe", vo, reads=[("xin", xb), ("Wm", hb, 2), ("Wm", hb, 3)], writes=[("ps", bk)])
                        S.add("act", lambda e: e.activation(
                            out=vpp[:, c_, 0:128], in_=ps[bk][:, 0:128], func=AF.Copy, scale=etok[:, 0, hc:hc + 1]),
                            reads=[("ps", bk), "etok"], writes=[("vpp", c_)])
                        S.add("act", lambda e: e.activation(
                            out=vpq[:, c_, 0:128], in_=ps[bk][:, 0:128], func=AF.Copy, scale=etok[:, 1, hc:hc + 1]),
                            reads=[("ps", bk), "etok"], writes=[("vpq", c_)])
                        S.add("act", lambda e: e.activation(out=og[:, c_, :], in_=ps[bk][:, 128:256], func=AF.Copy),
                              reads=[("ps", bk)], writes=[("og", c_)])
                        S.add("pool", lambda e: e.tensor_copy(out=vpp[:, c_, 128:129], in_=etok[:, 0, hc:hc + 1]),
                              reads=["etok", ("vpp", c_)], writes=[("vpp", c_)])
                        S.add("pool", lambda e: e.tensor_copy(out=vpq[:, c_, 128:129], in_=etok[:, 1, hc:hc + 1]),
                              reads=["etok", ("vpq", c_)], writes=[("vpq", c_)])

                    for tt in range(8):
                        xb = load_xin(tt)
                        for w in range(2):
                            bk = 5 + w
                            mm_group(ps[bk][:, :], [(Wm[hb][w][:, kc, :], xin[xb][:, kc, :]) for kc in range(8)],
                                     [("Wm", hb, w), ("xin", xb)], [("ps", bk)])
                            S.add("act", lambda e, bk=bk, w=w, tt=tt: e.activation(
                                out=xc[w][:, 3 + tt * 512:3 + (tt + 1) * 512], in_=ps[bk][:, :], func=AF.Copy),
                                reads=[("ps", bk)], writes=[("xc", w, tt)])
                        for sub in range(4):
                            vo_sub(xb, tt, sub)
                        if tt % 2 == 1:
                            conv_piece(1, tt // 2)
                            conv_piece(0, tt // 2)
                    for g8 in range(4):
                        ogk = [("og", c_) for c_ in range(g8 * 8, g8 * 8 + 8)]
                        S.add("act", lambda e, g8=g8: e.activation(out=og[:, g8 * 8:(g8 + 1) * 8, :], in_=og[:, g8 * 8:(g8 + 1) * 8, :],
                                                                   func=AF.Sigmoid), reads=ogk, writes=ogk)
                        S.add("pool", lambda e, g8=g8: e.tensor_tensor(
                            out=og[:, g8 * 8:(g8 + 1) * 8, :], in0=og[:, g8 * 8:(g8 + 1) * 8, :],
                            in1=gm[:, h * 128:(h + 1) * 128].unsqueeze(1).to_broadcast([128, 8, 128]), op=ALU.mult),
                            reads=ogk + ["const"], writes=ogk)
                    for g8 in range(4):
                        kbk_ = 7 if g8 % 2 == 0 else 4

                        def trk(e, g8=g8, kbk_=kbk_):
                            ins = None
                            for q in range(8):
                                c_ = g8 * 8 + q
                                ins = e.transpose(out=psb[kbk_][:, q * 128:(q + 1) * 128], in_=qkm[1][:, c_ * 128:(c_ + 1) * 128],
                                                  identity=identb[:])
                            return ins
                        S.add("pe", trk, reads=[("qkm", 1, g8), "const"], writes=[("ps", kbk_)])
                        S.add("act", lambda e, g8=g8, kbk_=kbk_: e.activation(
                            out=ktok[:, g8 * 8:(g8 + 1) * 8, :], in_=psb[kbk_].rearrange("p (q c) -> p q c", q=8),
                            func=AF.Copy, scale=inv_sqrt_dh), reads=[("ps", kbk_)], writes=[("ktok", g8)])

                    def st_mm(c_):
                        csl = slice(c_ * 128, (c_ + 1) * 128)
                        pi_ = c_ % 2
                        mm_group(ps[pi_][:, 0:128], [(qkm[1][:, csl], qkm[0][:, csl])],
                                 [("qkm", 0, c_ // 8), ("qkm", 1, c_ // 8)], [("ps", pi_)])
                        S.add("dve", lambda e: e.tensor_tensor(
                            out=PTm[pi_][:], in0=ps[pi_][:, 0:128], in1=mmask[:], op=ALU.mult),
                            reads=[("ps", pi_), "const"], writes=[("PTm", pi_)])

                    def core(c_):
                        csl = slice(c_ * 128, (c_ + 1) * 128)
                        hc = h * 32 + c_
                        pi_ = c_ % 2
                        obk = 2 + c_ % 2
                        ubk = 6 if c_ % 2 == 0 else 4
                        cur, nxt = c_ % 2, (c_ + 1) % 2
                        if c_ + 1 < 32:
                            st_mm(c_ + 1)
                        S.add("pe", lambda e: e.matmul(ps[obk][:, 0:129], lhsT=PTm[pi_][:], rhs=vpp[:, c_, :], start=True, stop=(c_ == 0)),
                              reads=[("PTm", pi_), ("vpp", c_)], writes=[("ps", obk)])
                        if c_ < 31:
                            mm_group(ps[ubk][:, 0:129], [(ktok[:, c_, :], vpq[:, c_, :])], [("ktok", c_ // 8), ("vpq", c_)], [("ps", ubk)])
                        if c_ > 0:
                            S.add("pe", lambda e: e.matmul(ps[obk][:, 0:129], lhsT=qkm[0][:, csl], rhs=Cb[cur][:], start=False, stop=True),
                                  reads=[("qkm", 0, c_ // 8), ("Cb", cur)], writes=[("ps", obk)])
                        if c_ < 31:
                            if c_ == 0:
                                S.add("dve", lambda e: e.tensor_copy(out=Cb[nxt][:], in_=ps[ubk][:, 0:129]),
                                      reads=[("ps", ubk)], writes=[("Cb", nxt)])
                                S.add("act", lambda e: e.activation(out=C32[nxt][:], in_=ps[ubk][:, 0:129], func=AF.Copy),
                                      reads=[("ps", ubk)], writes=[("C32", nxt)])
                            else:
                                S.add("dve", lambda e: e.scalar_tensor_tensor(
                                    out=Cb[nxt][:], in0=C32[cur][:], scalar=dec_bc[:, hc:hc + 1], in1=ps[ubk][:, 0:129],
                                    op0=ALU.mult, op1=ALU.add), reads=[("ps", ubk), ("C32", cur), "dec_bc"], writes=[("Cb", nxt)])
                                S.add("dve", lambda e: e.scalar_tensor_tensor(
                                    out=C32[nxt][:], in0=C32[cur][:], scalar=dec_bc[:, hc:hc + 1], in1=ps[ubk][:, 0:129],
                                    op0=ALU.mult, op1=ALU.add), reads=[("ps", ubk), ("C32", cur), "dec_bc"], writes=[("C32", nxt)])

                    def out_a1(c_):
                        obk = 2 + c_ % 2
                        oc = Ocp[c_ % 4]
                        S.add("act", lambda e: e.activation(out=oc[:], in_=ps[obk][:, 0:129], func=AF.Copy),
                              reads=[("ps", obk)], writes=[("Ocp", c_ % 4)])
                        S.add("act", lambda e: e.activation(
                            out=stm[:, c_, 6:7], in_=oc[:, 128:129], func=AF.Abs), reads=[("Ocp", c_ % 4)], writes=[(("stm", c_), 6)])

                    def out_d1(c_):
                        hc = h * 32 + c_
                        sk_ = ("stm", c_)
                        S.add("dve", lambda e: e.tensor_tensor(
                            out=stm[:, c_, 0:1], in0=stm[:, c_, 6:7], in1=etok[:, 2, hc:hc + 1], op=ALU.max),
                            reads=[(sk_, 6), "etok"], writes=[(sk_, 0)])
                        S.add("dve", lambda e: e.reciprocal(out=stm[:, c_, 1:2], in_=stm[:, c_, 0:1]),
                              reads=[(sk_, 0)], writes=[(sk_, 1)])

                    def out_a2(c_):
                        oc = Ocp[c_ % 4]
                        sk_ = ("stm", c_)

                        S.add("act", lambda e: e.activation(out=junk[:, 0:128], in_=oc[:, 0:128], func=AF.Square, scale=stm[:, c_, 1:2],
                                                            accum_out=stm[:, c_, 2:3]),
                              reads=[("Ocp", c_ % 4), (sk_, 1)], writes=[(sk_, 20)])
                        S.add("act", lambda e: e.activation(out=stm[:, c_, 7:8], in_=stm[:, c_, 2:3], func=AF.Copy),
                              reads=[(sk_, 20)], writes=[(sk_, 2)])
                        S.add("act", lambda e: e.activation(out=stm[:, c_, 3:4], in_=stm[:, c_, 2:3], func=AF.Sqrt,
                                                            scale=1.0 / 128, bias=EPS), reads=[(sk_, 2)], writes=[(sk_, 3)])

                    def out_d2(c_):
                        oc = Ocp[c_ % 4]
                        pi_ = c_ % 2
                        sk_ = ("stm", c_)
                        S.add("dve", lambda e: e.reciprocal(out=stm[:, c_, 4:5], in_=stm[:, c_, 3:4]),
                              reads=[(sk_, 3)], writes=[(sk_, 4)])
                        S.add("dve", lambda e: e.tensor_tensor(out=stm[:, c_, 5:6], in0=stm[:, c_, 4:5], in1=stm[:, c_, 1:2], op=ALU.mult),
                              reads=[(sk_, 4), (sk_, 1)], writes=[(sk_, 5)])
                        S.add("dve", lambda e: e.scalar_tensor_tensor(
                            out=om[pi_][:], in0=oc[:, 0:128], scalar=stm[:, c_, 5:6], in1=og[:, c_, :],
                            op0=ALU.mult, op1=ALU.mult), reads=[("Ocp", c_ % 4), (sk_, 5), ("og", c_)], writes=[("om", pi_)])

                    def out_t(c_):
                        pi_ = c_ % 2
                        q4 = c_ % 4
                        tb_ = (c_ // 4) % 2
                        S.add("pe", lambda e: e.transpose(out=psb[7][:, q4 * 128:(q4 + 1) * 128], in_=om[pi_][:], identity=identb[:]),
                              reads=[("om", pi_), "const"], writes=[("ps", 7)])
                        if q4 == 3:
                            S.add("act", lambda e: e.activation(out=omT[tb_][:], in_=psb[7][:, 0:512], func=AF.Copy),
                                  reads=[("ps", 7)], writes=[("omT", tb_)])
                            t0 = (c_ // 4) * 512
                            S.dma("sp", mixT_d[512 + h * 128:512 + (h + 1) * 128, t0:t0 + 512], omT[tb_][:],
                                  reads=[("omT", tb_)], writes=[("mixT_d", c_ // 4)])

                    st_mm(0)
                    for s_ in range(32 + 4):
                        if s_ < 32:
                            core(s_)
                        for fn_, lag in ((out_a1, 1), (out_d1, 2), (out_a2, 2), (out_d2, 3), (out_t, 4)):
                            if 0 <= s_ - lag < 32:
                                fn_(s_ - lag)

                for h in range(4):
                    mlstm_head(h)
        S.barrier()

        stg.close()
        with contextlib.ExitStack() as st4:
            sb4 = lambda n, sh, dt=F32: st4.enter_context(nc.sbuf_tensor("sb_" + n, sh, dt))
            if 4 in phases:
                Wout = sb4("Wout", [128, 8, DM], BF16)
                Wpg = sb4("Wpg", [128, 8, DM], BF16)
                Wple = sb4("Wple", [128, 2, DM], BF16)
                WU = [sb4("WU%d" % i, [128, 8, 512], BF16) for i in range(3)]
                WD = [sb4("WD%d" % i, [128, 4, 512], BF16) for i in range(3)]
                hid = sb4("hid", [128, 32, 512], BF16)
                rl = [sb4("rl%d" % i, [128, 512], BF16) for i in range(2)]
                xt4s = [sb4("xt4_%d" % i, [128, 4, DM]) for i in range(2)]
                mTs = [sb4("mT%d" % i, [128, 8, 512], BF16) for i in range(2)]
                uT = sb4("uT", [128, 8, 512], BF16)
                ub2 = [sb4("ub2_%d" % i, [128, DM], BF16) for i in range(2)]
                pt4s = [sb4("pt4_%d" % i, [128, 4, 256]) for i in range(2)]
                pb = sb4("pb", [128, 4, 256], BF16)
                pTt = sb4("pTt", [128, 2, 512], BF16)
                gt = [sb4("gt%d" % i, [128, 512]) for i in range(2)]
                tm = [sb4("tm%d" % i, [128, 512]) for i in range(2)]
                st2t = sb4("st2t", [128, 8 * 4 * 3 * 3])
                S.dma("pool", Wout[:], w_out.rearrange("(kc p) n -> p kc n", p=128), writes=["Wout"])
                S.dma("pool", Wpg[:], w_pg.rearrange("(kc p) n -> p kc n", p=128), writes=["Wpg"])
                S.dma("pool", Wple[:], w_ple.rearrange("(kc p) n -> p kc n", p=128), writes=["Wple"])
                w_up_v = w_up.rearrange("(kc p) n -> p kc n", p=128)
                w_down_v = w_down.rearrange("(fc p) n -> p fc n", p=128)
                wc = {"u": 0, "d": 0, "b": 0}

                def norm_to_uT(stt, which, gcol, xt4, XK):
                    for s_ in range(4):
                        base = ((stt * 4 + s_) * 3 + which) * 3
                        rms_rstd(xt4[:, s_, :], base, base + 1, base + 2, st2t, [XK(s_)], ("n2", stt, s_, which))
                    for s_ in range(4):
                        base = ((stt * 4 + s_) * 3 + which) * 3
                        tag = ("n2", stt, s_, which)
                        u_ = s_ % 2
                        S.add("act", lambda e, s_=s_, u_=u_, base=base: e.activation(
                            out=ub2[u_][:], in_=xt4[:, s_, :], func=AF.Copy, scale=st2t[:, base + 2:base + 3]),
                            reads=[XK(s_), (tag, "r")], writes=[("ub2", u_)])
                        bk = 2 + u_

                        def tr(e, u_=u_, bk=bk):
                            ins = None
                            for kc in range(8):
                                ins = e.transpose(out=psb[bk][:, kc * 128:(kc + 1) * 128],
                                                  in_=ub2[u_][:, kc * 128:(kc + 1) * 128], identity=identb[:])
                            return ins
                        S.add("pe", tr, reads=[("ub2", u_), "const"], writes=[("ps", bk)])
                        S.add("dve", lambda e, s_=s_, bk=bk, gcol=gcol: e.tensor_tensor(
                            out=uT[:, :, s_ * 128:(s_ + 1) * 128], in0=psb[bk].rearrange("p (c t) -> p c t", c=8),
                            in1=gT[:, gcol:gcol + 8].unsqueeze(2).to_broadcast([128, 8, 128]), op=ALU.mult),
                            reads=[("ps", bk), "const"], writes=[("uT", s_)])

                def load_st(stt):
                    b = stt % 2
                    t0 = stt * 512
                    S.dma("sp", xt4s[b][:], x[t0:t0 + 512, :].rearrange("(s p) d -> p s d", p=128),
                          writes=[("xt4", b, s_) for s_ in range(4)])
                    S.dma("sp", mTs[b][:], mixT_v[:, :, t0:t0 + 512], reads=[("mixT_d", stt)], writes=[("mT", b)])
                    S.dma("sp", pt4s[b][:], pin[t0:t0 + 512, :].rearrange("(s p) d -> p s d", p=128), writes=[("pt4", b)])

                def supertile(stt):
                    b = stt % 2
                    t0 = stt * 512
                    xt4, mT, pt4 = xt4s[b], mTs[b], pt4s[b]
                    XK = lambda s_: ("xt4", b, s_)
                    for s_ in range(4):
                        for hf in range(2):
                            bk = wc["b"] % 2
                            wc["b"] += 1
                            mm_group(ps[bk][:, :], [(mT[:, kc, s_ * 128:(s_ + 1) * 128], Wout[:, kc, hf * 512:(hf + 1) * 512]) for kc in range(8)],
                                     [("mT", b), "Wout"], [("ps", bk)])
                            S.add("dve", lambda e, s_=s_, hf=hf, bk=bk: e.tensor_tensor(
                                out=xt4[:, s_, hf * 512:(hf + 1) * 512], in0=ps[bk][:, :], in1=xt4[:, s_, hf * 512:(hf + 1) * 512], op=ALU.add),
                                reads=[("ps", bk), XK(s_)], writes=[XK(s_)])
                    if stt + 1 < 8:
                        load_st(stt + 1)
                    norm_to_uT(stt, 0, 8, xt4, XK)
                    for g in range(8):
                        ws = wc["u"] % 3
                        wc["u"] += 1
                        S.dma("pool", WU[ws][:], w_up_v[:, :, g * 512:(g + 1) * 512], writes=[("WU", ws)])
                        for q in range(4):
                            fc = g * 4 + q
                            bk = wc["b"] % 2
                            wc["b"] += 1
                            mm_group(ps[bk][:, :], [(WU[ws][:, kc, q * 128:(q + 1) * 128], uT[:, kc, :]) for kc in range(8)],
                                     [("WU", ws)] + [("uT", s_) for s_ in range(4)], [("ps", bk)])
                            S.add("act", lambda e, bk=bk: e.activation(out=rl[bk][:], in_=ps[bk][:, :], func=AF.Relu),
                                  reads=[("ps", bk)], writes=[("rl", bk)])
                            S.add("dve", lambda e, bk=bk, fc=fc: e.tensor_tensor(out=hid[:, fc, :], in0=rl[bk][:], in1=rl[bk][:], op=ALU.mult),
                                  reads=[("rl", bk)], writes=[("hid", fc)])
                    for hf in range(2):
                        for g in range(8):
                            ws = wc["d"] % 3
                            wc["d"] += 1
                            S.dma("pool", WD[ws][:], w_down_v[:, g * 4:(g + 1) * 4, hf * 512:(hf + 1) * 512], writes=[("WD", ws)])
                            for s_ in range(4):
                                def dn(e, ws=ws, s_=s_, g=g):
                                    ins = None
                                    for q in range(4):
                                        ins = e.matmul(ps[4 + s_][:, :], lhsT=hid[:, g * 4 + q, s_ * 128:(s_ + 1) * 128], rhs=WD[ws][:, q, :],
                                                       start=(g == 0 and q == 0), stop=(g == 7 and q == 3))
                                    return ins
                                S.add("pe", dn, reads=[("WD", ws)] + [("hid", g * 4 + q) for q in range(4)], writes=[("ps", 4 + s_)])
                        for s_ in range(4):
                            S.add("dve", lambda e, s_=s_, hf=hf: e.tensor_tensor(
                                out=xt4[:, s_, hf * 512:(hf + 1) * 512], in0=ps[4 + s_][:, :], in1=xt4[:, s_, hf * 512:(hf + 1) * 512], op=ALU.add),
                                reads=[("ps", 4 + s_), XK(s_)], writes=[XK(s_)])
                    norm_to_uT(stt, 1, 16, xt4, XK)
                    S.add("act", lambda e: e.activation(out=pb[:].rearrange("p s d -> p (s d)"), in_=pt4[:].rearrange("p s d -> p (s d)"), func=AF.Copy),
                          reads=[("pt4", b)], writes=["pb"])

                    def trp(e):
                        ins = None
                        for s_ in range(4):
                            for pc in range(2):
                                ins = e.transpose(out=psb[2][:, pc * 512 + s_ * 128:pc * 512 + (s_ + 1) * 128],
                                                  in_=pb[:, s_, pc * 128:(pc + 1) * 128], identity=identb[:])
                        return ins
                    S.add("pe", trp, reads=["pb", "const"], writes=[("ps", 2)])
                    S.add("act", lambda e: e.activation(out=pTt[:].rearrange("p a b -> p (a b)"), in_=psb[2][:, :], func=AF.Copy),
                          reads=[("ps", 2)], writes=["pTt"])
                    for s_ in range(4):
                        for hf in range(2):
                            bk = wc["b"] % 2
                            wc["b"] += 1
                            hsl = slice(hf * 512, (hf + 1) * 512)
                            mm_group(ps[bk][:, :], [(uT[:, kc, s_ * 128:(s_ + 1) * 128], Wpg[:, kc, hsl]) for kc in range(8)],
                                     [("uT", s_), "Wpg"], [("ps", bk)])
                            S.add("act", lambda e, bk=bk: e.activation(out=gt[bk][:], in_=ps[bk][:, :], func=AF.Sigmoid),
                                  reads=[("ps", bk)], writes=[("gt", bk)])
                            mm_group(ps[2 + bk][:, :], [(pTt[:, pc, s_ * 128:(s_ + 1) * 128], Wple[:, pc, hsl]) for pc in range(2)],
                                     ["pTt", "Wple"], [("ps", 2 + bk)])
                            S.add("dve", lambda e, bk=bk: e.tensor_tensor(out=tm[bk][:], in0=ps[2 + bk][:, :], in1=gt[bk][:], op=ALU.mult),
                                  reads=[("ps", 2 + bk), ("gt", bk)], writes=[("tm", bk)])
                            S.add("dve", lambda e, bk=bk, s_=s_, hsl=hsl: e.tensor_tensor(
                                out=xt4[:, s_, hsl], in0=tm[bk][:], in1=xt4[:, s_, hsl], op=ALU.add),
                                reads=[("tm", bk), XK(s_)], writes=[XK(s_)])
                    for s_ in range(4):
                        base = ((stt * 4 + s_) * 3 + 2) * 3
                        tag = ("n2", stt, s_, 2)
                        rms_rstd(xt4[:, s_, :], base, base + 1, base + 2, st2t, [XK(s_)], tag)
                        S.add("dve", lambda e, s_=s_, base=base: e.scalar_tensor_tensor(
                            out=xt4[:, s_, :], in0=xt4[:, s_, :], scalar=st2t[:, base + 2:base + 3], in1=gfin[:],
                            op0=ALU.mult, op1=ALU.mult), reads=[XK(s_), (tag, "r"), "const"], writes=[XK(s_)])
                        r0 = t0 + s_ * 128
                        S.dma("sp", out[r0:r0 + 128, :], xt4[:, s_, :], reads=[XK(s_)], writes=[("out", stt, s_)])
                load_st(0)
                for stt in range(8):
                    supertile(stt)
        S.emit()
    return nc


def _consts():
    bf = ml_dtypes.bfloat16
    c = {}
    c["identb"] = np.eye(128, dtype=np.float32).astype(bf)
    c["identf"] = np.eye(128, dtype=np.float32)
    pidx = np.arange(128)[:, None]
    fidx = np.arange(128)[None, :]
    prev = (pidx >= fidx).astype(np.float32)
    cur = (pidx <= fidx).astype(np.float32)
    c["amask"] = np.concatenate([prev, cur], axis=1).astype(bf)
    c["amask2"] = np.concatenate([cur, prev], axis=1).astype(bf)
    c["mmask"] = (cur / math.sqrt(128.0)).astype(np.float32)
    rm = np.zeros((128, 128), np.float32)
    cosT = np.ones((128, SEQ), np.float32)
    sinT = np.zeros((128, SEQ), np.float32)
    half = 8
    inv_freq = np.power(np.float32(500000.0), -np.arange(half, dtype=np.float32) / np.float32(half)).astype(np.float32)
    pos = np.arange(SEQ, dtype=np.float32)
    ang = (pos[None, :] * inv_freq[:, None]).astype(np.float32)
    for hh in range(2):
        for i in range(half):
            m1 = hh * 64 + i
            m2 = hh * 64 + i + half
            rm[m2, m1] = 1.0
            rm[m1, m2] = 1.0
            cosT[m1] = np.cos(ang[i]); cosT[m2] = np.cos(ang[i])
            sinT[m1] = -np.sin(ang[i]); sinT[m2] = np.sin(ang[i])
    c["rm"] = rm.astype(bf)
    c["ropec"] = cosT
    c["ropes"] = sinT
    lm = np.zeros((128, 128), np.float32)
    for h in range(4):
        for c1 in range(32):
            for c2 in range(c1 + 1, 32):
                lm[h * 32 + c1, h * 32 + c2] = 1.0
    c["lmat"] = lm
    return c


_NC_CACHE = {}


def _prep_shared(inp):
    f = lambda a: np.ascontiguousarray(np.asarray(a, dtype=np.float32))
    sh = {}
    sh["w_in"] = f(inp["w_in"][0])
    sh["w_out"] = f(inp["w_out"][0])
    sh["w_up"] = f(inp["w_up"][0])
    sh["w_down"] = f(inp["w_down"][0])
    sh["w_pg"] = f(inp["w_ple_gate"][0])
    sh["w_ple"] = f(inp["w_ple"][0])
    g1 = f(inp["norm_mix_g"][0]).reshape(8, 128).T
    g2 = f(inp["norm_mlp_g"][0]).reshape(8, 128).T
    g3 = f(inp["norm_ple_g"][0]).reshape(8, 128).T
    sh["gT"] = np.ascontiguousarray(np.concatenate([g1, g2, g3], axis=1))
    sh["gfin"] = np.ascontiguousarray(np.broadcast_to(f(inp["final_norm_g"])[None, :], (128, DM)))
    sh["gm"] = np.ascontiguousarray(np.broadcast_to(f(inp["mlstm_norm_g"][0])[None, :], (128, 512)))
    cwv = f(inp["conv_w"][0])
    sh["cw"] = np.ascontiguousarray(cwv.reshape(4, 8, 128).transpose(2, 1, 0).reshape(128, 32))
    sh["cb"] = np.ascontiguousarray(f(inp["conv_b"][0]).reshape(8, 128).T)
    gbv = f(inp["gate_b"][0])
    sh["gb"] = np.ascontiguousarray(np.stack([np.repeat(gbv[0:4], 32), np.repeat(gbv[4:8], 32)], axis=1))
    sh.update(_consts())
    return sh


def kernel(**inputs):
    if "nc" not in _NC_CACHE:
        _NC_CACHE["nc"] = build_nc()
    nc = _NC_CACHE["nc"]
    sh = _prep_shared(inputs)
    x = np.asarray(inputs["x"], dtype=np.float32)
    p = np.asarray(inputs["p"], dtype=np.float32)
    in_maps = []
    for b in range(8):
        m = dict(sh)
        m["x"] = np.ascontiguousarray(x[b])
        m["p"] = np.ascontiguousarray(p[0, b])
        in_maps.append(m)
    res = run_bass_kernel_spmd(nc, in_maps, core_ids=list(range(8)))
    return np.stack([np.asarray(r["out"], dtype=np.float32) for r in res.results], axis=0)
```

```python
import contextlib
import math
import numpy as np
import ml_dtypes
import concourse.bass as bass
import concourse.mybir as mybir
from concourse.bass_utils import run_bass_kernel_spmd

F32 = mybir.dt.float32
BF16 = mybir.dt.bfloat16
AF = mybir.ActivationFunctionType
ALU = mybir.AluOpType

SEQ = 4096
DM = 1024
NIN = 3592
EPS = 1e-6


class _Op:
    __slots__ = ("eng", "fn", "deps", "dma", "idx", "ms", "cnt", "sem", "prev_wait")

    def __init__(self, eng, fn, dma):
        self.eng = eng
        self.fn = fn
        self.dma = dma
        self.deps = set()
        self.ms = False
        self.cnt = 0
        self.sem = None
        self.prev_wait = 0


class Sched:
    ENGS = ("pe", "act", "dve", "pool", "sp")

    def __init__(self, nc, ndma_sems=12):
        self.nc = nc
        self.ops = []
        self.lastw = {}
        self.readers = {}
        self.ndma = ndma_sems
        self.last_on = {e: None for e in self.ENGS}
        self.dma_n = {"sp": 0, "pool": 0, "act": 0}
        self.last_dma = {}
        self.defer = None

    def _add(self, op, reads, writes):
        op.idx = len(self.ops)
        writes = list(writes) + [k for k in reads if isinstance(k, tuple) and k[0] == "ps" and k not in writes]
        deps = set()
        for k in reads:
            w = self.lastw.get(k)
            if w is not None:
                deps.add(w)
        for k in writes:
            w = self.lastw.get(k)
            if w is not None:
                deps.add(w)
            for r in self.readers.get(k, ()):
                deps.add(r)
        deps.discard(op.idx)
        for k in reads:
            self.readers.setdefault(k, []).append(op.idx)
        for k in writes:
            self.lastw[k] = op.idx
            self.readers[k] = []
        if op.eng == "pe" and not op.dma:
            deps = {d for d in deps if not (self.ops[d].eng == "pe" and not self.ops[d].dma)}
        op.deps = deps
        self.ops.append(op)
        if not op.dma:
            self.last_on[op.eng] = op.idx
        else:
            n = self.dma_n[op.eng]
            self.dma_n[op.eng] = n + 1
            self.last_dma[(op.eng, n % self.ndma)] = op.idx
        return op

    def add(self, eng, fn, reads=(), writes=()):
        if self.defer is not None:
            self.defer.append((eng, fn, False, list(reads), list(writes)))
            return None
        return self._add(_Op(eng, fn, False), reads, writes)

    def dma(self, eng, out, in_, reads=(), writes=(), **kw):
        def fn(e):
            return e.dma_start(out=out, in_=in_, **kw)
        if self.defer is not None:
            self.defer.append((eng, fn, True, list(reads), list(writes)))
            return None
        return self._add(_Op(eng, fn, True), reads, writes)

    def begin_defer(self):
        self.defer = []

    def end_defer(self):
        q, self.defer = self.defer, None
        return {"ops": q, "live": set()}

    def flush_group(self, q):
        ops, live = q["ops"], q["live"]
        while ops:
            eng, fn, dma, reads, writes = ops.pop(0)
            if eng == "pe" and not dma:
                live.update(k for k in writes if isinstance(k, tuple) and k[0] == "ps")
            else:
                live.difference_update(k for k in reads if isinstance(k, tuple) and k[0] == "ps")
            self._add(_Op(eng, fn, dma), reads, writes)
            if not live:
                break

    def flush_all(self, q):
        while q["ops"]:
            self.flush_group(q)

    def barrier(self):
        allprev = set()
        for e in self.ENGS:
            if self.last_on[e] is not None:
                allprev.add(self.last_on[e])
        allprev.update(self.last_dma.values())
        for e in self.ENGS:
            op = _Op(e, None, False)
            op.idx = len(self.ops)
            op.deps = set(allprev)
            self.ops.append(op)

    def emit(self):
        nc = self.nc
        ops = self.ops
        for o in ops:
            for d in o.deps:
                ops[d].ms = True
        with contextlib.ExitStack() as st:
            esem = {e: st.enter_context(nc.semaphore("s_" + e)) for e in self.ENGS}
            dsem = {e: [st.enter_context(nc.semaphore("d_%s%d" % (e, i))) for i in range(self.ndma)]
                    for e in ("sp", "pool", "act")}
            cnt = {e: 0 for e in self.ENGS}
            dn = {e: 0 for e in dsem}
            final_dma = {}
            for o in ops:
                if o.dma:
                    n = dn[o.eng]
                    dn[o.eng] += 1
                    j = n % self.ndma
                    o.sem = dsem[o.eng][j]
                    o.cnt = 16 * (n // self.ndma + 1)
                    o.prev_wait = 16 * (n // self.ndma)
                    final_dma[(o.eng, j)] = (o.sem, o.cnt)
                elif o.ms and o.fn is not None:
                    cnt[o.eng] += 1
                    o.cnt = cnt[o.eng]
                    o.sem = esem[o.eng]
            block = st.enter_context(nc.Block())

            def stream(ename, e):
                seen = {}

                def wait(sem, key, val):
                    if val <= 0 or seen.get(key, 0) >= val:
                        return
                    e.wait_ge(sem, val)
                    seen[key] = val

                for o in ops:
                    if o.eng != ename:
                        continue
                    for d in sorted(o.deps):
                        p = ops[d]
                        if p.fn is None:
                            continue
                        if p.dma:
                            wait(p.sem, id(p.sem), p.cnt)
                        else:
                            wait(p.sem, p.eng, p.cnt)
                    if o.fn is None:
                        continue
                    if o.dma:
                        wait(o.sem, id(o.sem), o.prev_wait)
                        o.fn(e).then_inc(o.sem, 16)
                    else:
                        ins = o.fn(e)
                        if o.ms:
                            ins.then_inc(o.sem, 1)
                if ename == "sp":
                    for (sem, c) in final_dma.values():
                        wait(sem, id(sem), c)

            @block.tensor
            def _(e):
                stream("pe", e)

            @block.scalar
            def _(e):
                stream("act", e)

            @block.vector
            def _(e):
                stream("dve", e)

            @block.gpsimd
            def _(e):
                stream("pool", e)

            @block.sync
            def _(e):
                stream("sp", e)


def build_nc(debug=False, phases=(1, 2, 3, 4)):
    nc = bass.Bass("TRN2", target_bir_lowering=False)
    S = Sched(nc)
    din = lambda n, sh, dt=F32: nc.dram_tensor(n, sh, dt, kind="ExternalInput").ap()
    x = din("x", [SEQ, DM])
    pin = din("p", [SEQ, 256])
    w_in = din("w_in", [DM, NIN])
    w_out = din("w_out", [DM, DM])
    w_up = din("w_up", [DM, 4096])
    w_down = din("w_down", [4096, DM])
    w_pg = din("w_pg", [DM, DM])
    w_ple = din("w_ple", [256, DM])
    gT_d = din("gT", [128, 24])
    gfin_d = din("gfin", [128, DM])
    gm_d = din("gm", [128, 512])
    cw_d = din("cw", [128, 32])
    cb_d = din("cb", [128, 8])
    gb_d = din("gb", [128, 2])
    identb_d = din("identb", [128, 128], BF16)
    identf_d = din("identf", [128, 128])
    amask_d = din("amask", [128, 256], BF16)
    amask2_d = din("amask2", [128, 256], BF16)
    mmask_d = din("mmask", [128, 128])
    rm_d = din("rm", [128, 128], BF16)
    ropec_d = din("ropec", [128, SEQ])
    ropes_d = din("ropes", [128, SEQ])
    lmat_d = din("lmat", [128, 128])
    out = nc.dram_tensor("out", [SEQ, DM], F32, kind="ExternalOutput").ap()
    sk = "ExternalOutput" if debug else "Internal"
    xT_d = nc.dram_tensor("xT_d", [DM, SEQ], BF16, kind=sk).ap()
    mixT_d = nc.dram_tensor("mixT_d", [DM, SEQ], BF16, kind=sk).ap()
    gsc = nc.dram_tensor("gsc", [8, SEQ], F32, kind=sk).ap()
    xT_v = xT_d.rearrange("(kc p) t -> p kc t", p=128)
    mixT_v = mixT_d.rearrange("(kc p) t -> p kc t", p=128)
    w_in_v = w_in.rearrange("(kc p) n -> p kc n", p=128)

    with contextlib.ExitStack() as st:
        sb = lambda n, sh, dt=F32: st.enter_context(nc.sbuf_tensor("sb_" + n, sh, dt))
        ps = [st.enter_context(nc.psum_tensor("ps%d" % i, [128, 512], F32)) for i in range(8)]
        psb = [t[:].bitcast(BF16) for t in ps]

        identb = sb("identb", [128, 128], BF16)
        identf = sb("identf", [128, 128])
        amask = sb("amask", [128, 256], BF16)
        amask2 = sb("amask2", [128, 256], BF16)
        mmask = sb("mmask", [128, 128])
        rm = sb("rm", [128, 128], BF16)
        lmat = sb("lmat", [128, 128])
        gT = sb("gT", [128, 24])
        gfin = sb("gfin", [128, DM])
        gm = sb("gm", [128, 512])
        cw = sb("cw", [128, 32])
        cb = sb("cb", [128, 8])
        gb = sb("gb", [128, 2])
        onesf = sb("onesf", [128, 128])
        junk = sb("junk", [128, DM], BF16)
        dummy = sb("dummy", [128, 2])
        ckeys = []
        for t_, d_ in ((identb, identb_d), (identf, identf_d), (amask, amask_d), (amask2, amask2_d), (mmask, mmask_d), (rm, rm_d),
                       (lmat, lmat_d), (gT, gT_d), (gfin, gfin_d), (gm, gm_d), (cw, cw_d), (cb, cb_d), (gb, gb_d)):
            S.dma("sp", t_[:], d_, writes=[("c", t_.name)])
            ckeys.append(("c", t_.name))
        S.add("pool", lambda e: e.memset(onesf[:], 1.0), writes=[("c", "ones")])
        S.add("pool", lambda e: e.memset(dummy[:], 0.0), reads=ckeys + [("c", "ones")], writes=["const"])

        def mm_group(out_ap, pairs, reads, writes, **kw):
            def fn(e):
                n = len(pairs)
                ins = None
                for i, (l, r) in enumerate(pairs):
                    ins = e.matmul(out_ap, lhsT=l, rhs=r, start=(i == 0), stop=(i == n - 1), **kw)
                return ins
            S.add("pe", fn, reads=list(reads) + ["const"], writes=writes)

        cnt = {"xin": 0}
        xin = [None, None]

        def load_xin(tt):
            b = cnt["xin"] % 2
            cnt["xin"] += 1
            S.dma("sp", xin[b][:], xT_v[:, :, tt * 512:(tt + 1) * 512], reads=[("xT_d", tt)], writes=[("xin", b)])
            return b

        def rms_rstd(src_ap, col_ss, col_sq, col_r, stat, rkeys, tag, n=DM):
            S.add("act", lambda e: e.activation(out=junk[:, 0:n], in_=src_ap, func=AF.Square,
                                                accum_out=stat[:, col_ss:col_ss + 1]),
                  reads=rkeys, writes=[(tag, "ss0")])
            S.add("act", lambda e: e.activation(out=stat[:, col_sq:col_sq + 1], in_=stat[:, col_ss:col_ss + 1], func=AF.Copy),
                  reads=[(tag, "ss0")], writes=[(tag, "ss")])
            S.add("act", lambda e: e.activation(out=stat[:, col_sq:col_sq + 1], in_=stat[:, col_ss:col_ss + 1],
                                                func=AF.Sqrt, scale=1.0 / n, bias=EPS),
                  reads=[(tag, "ss")], writes=[(tag, "sq")])
            S.add("dve", lambda e: e.reciprocal(out=stat[:, col_r:col_r + 1], in_=stat[:, col_sq:col_sq + 1]),
                  reads=[(tag, "sq")], writes=[(tag, "r")])

        stg = contextlib.ExitStack()
        sbg = lambda n, sh, dt=F32: stg.enter_context(nc.sbuf_tensor("sb_" + n, sh, dt))
        Wg = sbg("Wg", [128, 8, 8], BF16)
        gst = [sbg("gst%d" % i, [128, 2, 512]) for i in range(2)]
        G = {n_: sbg("G_" + n_, [128, 128]) for n_ in
             ("gI", "gF", "ef", "sp", "nFl", "nF", "gi", "a", "Ml", "e", "e2", "eps")}
        col = sbg("col", [128, 8])
        rows = sbg("rows", [128, 5, 128])
        dec_bc = sbg("dec_bc", [128, 128])
        etok = sbg("etok", [128, 3, 128])

        def gates_prologue():
            S.dma("pool", Wg[:], w_in_v[:, :, 3584:3592], writes=["Wg"])
            for tt in range(8):
                xb = load_xin(tt)
                tsl = slice(tt * 512, (tt + 1) * 512)
                gb_ = tt % 2
                for gi_ in range(2):
                    bk = 6 + gi_
                    mm_group(ps[bk][0:4, :], [(Wg[:, kc, gi_ * 4:gi_ * 4 + 4], xin[xb][:, kc, :]) for kc in range(8)],
                             ["Wg", ("xin", xb)], [("ps", bk)])
                    S.add("act", lambda e, gi_=gi_, bk=bk, gb_=gb_: e.activation(
                        out=gst[gb_][0:4, gi_, :], in_=ps[bk][0:4, :], func=AF.Copy),
                        reads=[("ps", bk)], writes=[("gst", gb_, gi_)])
                S.dma("sp", gsc.rearrange("(g h) t -> h g t", g=2)[:, :, tsl], gst[gb_][0:4, :, :],
                      reads=[("gst", gb_, 0), ("gst", gb_, 1)], writes=[("gsc", tt)])
            S.dma("sp", G["gI"][:], gsc[0:4, :].rearrange("h (c t) -> (h c) t", t=128), reads=[("gsc", t_) for t_ in range(8)], writes=["gI"])
            S.dma("sp", G["gF"][:], gsc[4:8, :].rearrange("h (c t) -> (h c) t", t=128), reads=[("gsc", t_) for t_ in range(8)], writes=["gF"])
            S.add("dve", lambda e: e.tensor_scalar(out=col[:, 0:2], in0=gb[:, 0:2], scalar1=-1.0, scalar2=None, op0=ALU.mult),
                  reads=["const"], writes=["ngb"])
            S.add("act", lambda e: e.activation(out=G["ef"][:], in_=G["gF"][:], func=AF.Exp, scale=-1.0, bias=col[:, 1:2]),
                  reads=["gF", "ngb"], writes=["ef"])
            S.add("act", lambda e: e.activation(out=G["sp"][:], in_=G["ef"][:], func=AF.Ln, scale=1.0, bias=1.0),
                  reads=["ef"], writes=["sp"])
            S.add("dve", lambda e: e.tensor_tensor_scan(out=G["nFl"][:], data0=onesf[:, 0:128], data1=G["sp"][:],
                                                        initial=0.0, op0=ALU.mult, op1=ALU.add),
                  reads=["sp", "const"], writes=["nFl"])
            mm_group(ps[6][:, 0:1], [(lmat[:], G["nFl"][:, 127:128])], ["nFl"], [("ps", 6)])
            S.add("act", lambda e: e.activation(out=col[:, 2:3], in_=ps[6][:, 0:1], func=AF.Copy),
                  reads=[("ps", 6)], writes=["carr"])
            S.add("dve", lambda e: e.tensor_scalar(out=G["nF"][:], in0=G["nFl"][:], scalar1=col[:, 2:3], scalar2=None, op0=ALU.add),
                  reads=["nFl", "carr"], writes=["nF"])
            S.add("act", lambda e: e.activation(out=G["gi"][:], in_=G["gI"][:], func=AF.Identity, bias=gb[:, 0:1]),
                  reads=["gI", "const"], writes=["gi"])
            S.add("dve", lambda e: e.tensor_tensor(out=G["a"][:], in0=G["gi"][:], in1=G["nF"][:], op=ALU.add),
                  reads=["gi", "nF"], writes=["a"])
            S.add("dve", lambda e: e.tensor_tensor_scan(out=G["Ml"][:], data0=G["a"][:], data1=G["a"][:],
                                                        initial=-1e30, op0=ALU.max, op1=ALU.max),
                  reads=["a"], writes=["Ml"])
            S.add("pe", lambda e: e.transpose(out=ps[7][0:1, 0:128], in_=G["Ml"][:, 127:128], identity=identf[:]),
                  reads=["Ml", "const"], writes=[("ps", 7)])
            S.add("act", lambda e: e.activation(out=rows[0:1, 0, :], in_=ps[7][0:1, 0:128], func=AF.Copy),
                  reads=[("ps", 7)], writes=["trow"])
            for h in range(4):
                S.add("dve", lambda e, h=h: e.tensor_tensor_scan(
                    out=rows[0:1, 1, h * 32:(h + 1) * 32], data0=rows[0:1, 0, h * 32:(h + 1) * 32],
                    data1=rows[0:1, 0, h * 32:(h + 1) * 32], initial=0.0, op0=ALU.max, op1=ALU.max),
                    reads=["trow"], writes=[("Irow", h)])
            S.add("pool", lambda e: e.memset(rows[0:1, 2, :], 0.0), writes=["Rrow"])
            S.add("dve", lambda e: e.tensor_copy(out=rows[0:1, 2, :].rearrange("p (h c) -> p h c", h=4)[:, :, 1:32],
                                                 in_=rows[0:1, 1, :].rearrange("p (h c) -> p h c", h=4)[:, :, 0:31]),
                  reads=[("Irow", h) for h in range(4)] + ["Rrow"], writes=["Rrow"])
            S.add("dve", lambda e: e.tensor_tensor(out=rows[0:1, 3, :], in0=rows[0:1, 2, :], in1=rows[0:1, 1, :], op=ALU.subtract),
                  reads=["Rrow"] + [("Irow", h) for h in range(4)], writes=["tmprow"])
            S.add("act", lambda e: e.activation(out=rows[0:1, 4, :], in_=rows[0:1, 3, :], func=AF.Exp),
                  reads=["tmprow"], writes=["drow"])

            def colmm(e):
                e.matmul(ps[6][:, 0:1], lhsT=rows[0:1, 2, :], rhs=onesf[0:1, 0:1], start=True, stop=True)
                return e.matmul(ps[6][:, 1:2], lhsT=rows[0:1, 1, :], rhs=onesf[0:1, 0:1], start=True, stop=True)
            S.add("pe", colmm, reads=["Rrow", "const"] + [("Irow", h) for h in range(4)], writes=[("ps", 6)])
            S.add("dve", lambda e: e.tensor_scalar(out=col[:, 4:6], in0=ps[6][:, 0:2], scalar1=-1.0, scalar2=None, op0=ALU.mult),
                  reads=[("ps", 6)], writes=["nrho"])
            mm_group(ps[7][:, 0:128], [(onesf[0:1, 0:128], rows[0:1, 4, :])], ["drow"], [("ps", 7)])
            S.add("act", lambda e: e.activation(out=dec_bc[:], in_=ps[7][:, 0:128], func=AF.Copy),
                  reads=[("ps", 7)], writes=["dec_bc"])
            S.add("act", lambda e: e.activation(out=G["e"][:], in_=G["a"][:], func=AF.Exp, bias=col[:, 4:5]),
                  reads=["a", "nrho"], writes=["e"])
            S.add("act", lambda e: e.activation(out=G["e2"][:], in_=G["a"][:], func=AF.Exp, bias=col[:, 5:6]),
                  reads=["a", "nrho"], writes=["e2"])
            S.add("act", lambda e: e.activation(out=G["eps"][:], in_=G["nF"][:], func=AF.Exp, bias=col[:, 4:5]),
                  reads=["nF", "nrho"], writes=["epsm"])

            def tr3(e):
                e.transpose(out=ps[6][:, 0:128], in_=G["e"][:], identity=identf[:])
                e.transpose(out=ps[6][:, 128:256], in_=G["e2"][:], identity=identf[:])
                return e.transpose(out=ps[6][:, 256:384], in_=G["eps"][:], identity=identf[:])
            S.add("pe", tr3, reads=["e", "e2", "epsm", "const", "nrho"], writes=[("ps", 6)])
            S.add("act", lambda e: e.activation(out=etok[:].rearrange("p a b -> p (a b)"), in_=ps[6][:, 0:384], func=AF.Copy),
                  reads=[("ps", 6)], writes=["etok"])


        with contextlib.ExitStack() as st1:
            sb1 = lambda n, sh, dt=F32: st1.enter_context(nc.sbuf_tensor("sb_" + n, sh, dt))
            xt = [sb1("xt%d" % i, [128, DM]) for i in range(6)]
            ub = [sb1("ub%d" % i, [128, DM], BF16) for i in range(2)]
            xTs = [sb1("xTs%d" % i, [128, 8, 512], BF16) for i in range(2)]
            st1t = sb1("st1t", [128, 96])
            if 1 in phases:
                def p1a_load(i):
                    b = i % 6
                    S.dma("sp", xt[b][:], x[i * 128:(i + 1) * 128, :], writes=[("xt", b)])
                    rms_rstd(xt[b][:], i, 32 + i, 64 + i, st1t, [("xt", b)], ("n1", i))

                def p1a_rest(i):
                    b = i % 6
                    u = i % 2
                    tt, sub = i // 4, i % 4
                    tb = tt % 2
                    S.add("dve", lambda e: e.tensor_scalar(out=ub[u][:], in0=xt[b][:], scalar1=st1t[:, 64 + i:65 + i],
                                                           scalar2=None, op0=ALU.mult),
                          reads=[("xt", b), (("n1", i), "r")], writes=[("ub", u)])

                    def tr(e):
                        ins = None
                        for kc in range(8):
                            ins = e.transpose(out=psb[u][:, kc * 128:(kc + 1) * 128],
                                              in_=ub[u][:, kc * 128:(kc + 1) * 128], identity=identb[:])
                        return ins
                    S.add("pe", tr, reads=[("ub", u), "const"], writes=[("ps", u)])
                    S.add("dve", lambda e: e.tensor_tensor(
                        out=xTs[tb][:, :, sub * 128:(sub + 1) * 128],
                        in0=psb[u].rearrange("p (c t) -> p c t", c=8),
                        in1=gT[:, 0:8].unsqueeze(2).to_broadcast([128, 8, 128]), op=ALU.mult),
                        reads=[("ps", u), "const"], writes=[("xTs", tb, sub)])
                    if sub == 3:
                        S.dma("pool", xT_v[:, :, tt * 512:(tt + 1) * 512], xTs[tb][:],
                              reads=[("xTs", tb, s_) for s_ in range(4)], writes=[("xT_d", tt)])
                for i in range(5):
                    p1a_load(i)
                for i in range(32):
                    p1a_rest(i)
                    if i + 5 < 32:
                        p1a_load(i + 5)
        S.barrier()

        with contextlib.ExitStack() as st2:
            sb2 = lambda n, sh, dt=F32: st2.enter_context(nc.sbuf_tensor("sb_" + n, sh, dt))
            if 2 in phases:
                xin[0] = sb2("xin0b", [128, 8, 512], BF16)
                xin[1] = sb2("xin1b", [128, 8, 512], BF16)
                Wa = [[sb2("Wa%d_%d" % (i, w), [128, 8, 128], BF16) for w in range(3)] for i in range(2)]
                qT = sb2("qT", [128, SEQ], BF16)
                kTh = [sb2("kTA", [128, SEQ], BF16), sb2("kTB", [128, SEQ], BF16)]
                vT = sb2("vT", [128, SEQ], BF16)
                Vb = [sb2("Vb%d" % i, [128, 32, 2, 65], BF16) for i in range(3)]
                acc = [sb2("acc%d" % i, [128, SEQ]) for i in range(2)]
                tabs = [sb2("tabs%d" % i, [128, 2, 512]) for i in range(2)]
                qraw = [sb2("qraw%d" % i, [128, 512], BF16) for i in range(2)]
                t1 = [sb2("t1_%d" % i, [128, 512]) for i in range(2)]
                t2 = [sb2("t2_%d" % i, [128, 512]) for i in range(2)]
                NPT = 6
                Eb = [sb2("Eb%d" % i, [128, 256], BF16) for i in range(NPT)]
                PT = [sb2("PT%d" % i, [128, 256], BF16) for i in range(NPT)]
                rdt = sb2("rdt", [128, SEQ])
                onb = [sb2("onb%d" % i, [128, 512], BF16) for i in range(2)]
                S.add("pool", lambda e: e.memset(kTh[0][64:128, :], 0.0), writes=["kz"])
                S.add("pool", lambda e: e.memset(kTh[1][0:64, :], 0.0), writes=["kz"])
                for i in range(3):
                    S.add("pool", lambda e, i=i: e.memset(Vb[i][:], 1.0), writes=[("Vb", i, g) for g in range(4)])
                rc = {"r": 0, "s": 0, "o": 0, "e": 0, "n": 0}
                S.begin_defer()
                gates_prologue()
                gq = S.end_defer()
                for j in range(4):
                    jb = j % 2
                    for w in range(3):
                        c0 = w * 512 + j * 128
                        S.dma("pool", Wa[jb][w][:], w_in_v[:, :, c0:c0 + 128], writes=[("Wa", jb, w)])
                    for tt in range(8):
                        xb = load_xin(tt)
                        tb = tt % 2
                        tsl = slice(tt * 512, (tt + 1) * 512)
                        S.dma("sp", tabs[tb][:, 0, :], ropec_d[:, tsl], writes=[("tabs", tb, 0)])
                        S.dma("sp", tabs[tb][:, 1, :], ropes_d[:, tsl], writes=[("tabs", tb, 1)])
                        pbk = (5, 6) if tb == 0 else (0, 1)
                        rbk = (7, 3)
                        vbk = 2 if tb == 0 else 4
                        for w in range(2):
                            bk = pbk[w]
                            mm_group(ps[bk][:, :], [(Wa[jb][w][:, kc, :], xin[xb][:, kc, :]) for kc in range(8)],
                                     [("Wa", jb, w), ("xin", xb)], [("ps", bk)])
                            S.add("act", lambda e, w=w, bk=bk: e.activation(out=qraw[w][:], in_=ps[bk][:, :], func=AF.Copy),
                                  reads=[("ps", bk)], writes=[("qraw", w)])
                        mm_group(ps[vbk][:, :], [(Wa[jb][2][:, kc, :], xin[xb][:, kc, :]) for kc in range(8)],
                                 [("Wa", jb, 2), ("xin", xb)], [("ps", vbk)])
                        S.add("act", lambda e, vbk=vbk, tsl=tsl: e.activation(out=vT[:, tsl], in_=ps[vbk][:, :], func=AF.Copy),
                              reads=[("ps", vbk)], writes=[("vT", tt)])
                        for w in range(2):
                            bk = pbk[w]
                            mm_group(ps[rbk[w]][:, :], [(rm[:], qraw[w][:])], [("qraw", w)], [("ps", rbk[w])])
                            S.add("dve", lambda e, w=w, bk=bk, tb=tb: e.tensor_tensor(
                                out=t1[w][:], in0=ps[bk][:, :], in1=tabs[tb][:, 0, :], op=ALU.mult),
                                reads=[("ps", bk), ("tabs", tb, 0)], writes=[("t1", w)])
                            S.add("dve", lambda e, w=w, tb=tb: e.tensor_tensor(
                                out=t2[w][:], in0=ps[rbk[w]][:, :], in1=tabs[tb][:, 1, :], op=ALU.mult),
                                reads=[("ps", rbk[w]), ("tabs", tb, 1)], writes=[("t2", w)])
                            if w == 0:
                                S.add("pool", lambda e, tsl=tsl: e.tensor_tensor(
                                    out=qT[:, tsl], in0=t1[0][:], in1=t2[0][:], op=ALU.add),
                                    reads=[("t1", 0), ("t2", 0)], writes=[("qT", tt)])
                            else:
                                S.add("pool", lambda e, tsl=tsl: e.tensor_tensor(
                                    out=kTh[0][0:64, tsl], in0=t1[1][0:64, :], in1=t2[1][0:64, :], op=ALU.add),
                                    reads=[("t1", 1), ("t2", 1)], writes=[("kT", 0, tt)])
                                S.add("pool", lambda e, tsl=tsl: e.tensor_tensor(
                                    out=kTh[1][64:128, tsl], in0=t1[1][64:128, :], in1=t2[1][64:128, :], op=ALU.add),
                                    reads=[("t1", 1), ("t2", 1)], writes=[("kT", 1, tt)])
                    pats = ((0, 1), (1, 4), (2, 16))
                    import os as _os
                    _stg = int(_os.environ.get('ATT_STAGE', '9'))
                    if _stg < 2:
                        continue
                    for (pi, d) in pats:
                        nb = 32 // d
                        for g8 in range(4):
                            vbk_ = 6 + (g8 % 2)

                            def trv(e, pi=pi, d=d, nb=nb, g8=g8, vbk_=vbk_):
                                ins = None
                                for q in range(8):
                                    bi = g8 * 8 + q
                                    r_, n_ = bi // nb, bi % nb
                                    base = d * 128 * n_ + r_
                                    ins = e.transpose(out=psb[vbk_][:, q * 128:(q + 1) * 128],
                                                      in_=vT[:, base:base + 127 * d + 1:d], identity=identb[:])
                                return ins
                            S.add("pe", trv, reads=[("vT", t_) for t_ in range(8)] + ["const"], writes=[("ps", vbk_)])
                            S.add("act" if g8 % 2 == 0 else "dve", (lambda e, pi=pi, g8=g8, vbk_=vbk_: e.activation(
                                out=Vb[pi][:, g8 * 8:(g8 + 1) * 8, :, 0:64],
                                in_=psb[vbk_].rearrange("p (q h c) -> p q h c", q=8, h=2), func=AF.Copy)) if g8 % 2 == 0 else
                                (lambda e, pi=pi, g8=g8, vbk_=vbk_: e.tensor_copy(
                                    out=Vb[pi][:, g8 * 8:(g8 + 1) * 8, :, 0:64],
                                    in_=psb[vbk_].rearrange("p (q h c) -> p q h c", q=8, h=2))),
                                reads=[("ps", vbk_)], writes=[("Vb", pi, g8)])
                    if _stg < 3:
                        continue
                    jobs = []
                    for (pi, d) in pats[:_stg - 2]:
                        nb = 32 // d
                        for hh in range(2):
                            for gi in range(32):
                                r_, k_ = gi // nb, gi % nb
                                jobs.append(dict(pi=pi, d=d, nb=nb, hh=hh, gi=gi, r_=r_, k_=k_, has_next=(k_ + 1 < nb)))
                    gbank = {}

                    def bank_of(pi, hh, grp):
                        key = (pi, hh, grp)
                        if key not in gbank:
                            gbank[key] = [4 + rc["o"] % 2, False]
                            rc["o"] += 1
                        return gbank[key]

                    def stage_a(jb_):
                        d, hh, r_, k_ = jb_["d"], jb_["hh"], jb_["r_"], jb_["k_"]
                        base = d * 128 * k_ + r_
                        nq_ = 256 if jb_["has_next"] else 128
                        ksl = slice(base, base + 127 * d + 1, d)
                        qsl = slice(base, base + (nq_ - 1) * d + 1, d)
                        lo_t, hi_t = base // 512, (base + (nq_ - 1) * d) // 512
                        sb_ = rc["s"] % 4
                        rc["s"] += 1
                        S.add("pe", lambda e: e.matmul(ps[sb_][:, 0:nq_], lhsT=kTh[hh][:, ksl], rhs=qT[:, qsl], start=True, stop=True),
                              reads=[("qT", t_) for t_ in range(lo_t, hi_t + 1)] +
                              [("kT", hh, t_) for t_ in range(lo_t, hi_t + 1)] + ["kz"], writes=[("ps", sb_)])
                        eb = rc["e"] % NPT
                        rc["e"] += 1
                        jb_["eb"] = eb
                        S.add("act", lambda e: e.activation(
                            out=Eb[eb][:, 0:nq_], in_=ps[sb_][:, 0:nq_], func=AF.Exp, scale=0.125),
                            reads=[("ps", sb_)], writes=[("Eb", eb)])
                        meng = "pool" if (rc["e"] % 2 == 0) else "dve"
                        S.add(meng, lambda e: e.tensor_tensor(
                            out=PT[eb][:, 0:nq_], in0=Eb[eb][:, 0:nq_], in1=amask2[:, 0:nq_], op=ALU.mult),
                            reads=[("Eb", eb), "const"], writes=[("PT", eb)])

                    def evac(pi, hh, grp, pob):
                        if pi == 0:
                            S.add("dve", lambda e: e.tensor_copy(
                                out=acc[hh][0:65, grp * 512:(grp + 1) * 512], in_=ps[pob][0:65, :]),
                                reads=[("ps", pob)], writes=[("acc", hh, grp)])
                        elif pi == 1:
                            r_, h2 = grp // 2, grp % 2

                            def ad(e):
                                av = acc[hh][0:65, h2 * 2048:(h2 + 1) * 2048].rearrange("p (s j r) -> p r s j", s=4, r=4)[:, r_]
                                return e.tensor_tensor(out=av, in0=ps[pob][0:65, :].rearrange("p (s j) -> p s j", s=4),
                                                       in1=av, op=ALU.add)
                            ks = [("acc", hh, h2 * 4 + q_) for q_ in range(4)]
                            S.add("dve", ad, reads=[("ps", pob)] + ks, writes=ks)
                        else:
                            def ad3(e):
                                av = acc[hh][0:65, :].rearrange("p (k j r) -> p r k j", k=2, r=16)[:, 2 * grp:2 * grp + 2]
                                return e.tensor_tensor(out=av, in0=ps[pob][0:65, :].rearrange("p (r k j) -> p r k j", r=2, k=2),
                                                       in1=av, op=ALU.add)
                            ks = [("acc", hh, q_) for q_ in range(8)]
                            S.add("dve", ad3, reads=[("ps", pob)] + ks, writes=ks)

                    def stage_b(jb_):
                        pi, hh, gi, eb, has_next = jb_["pi"], jb_["hh"], jb_["gi"], jb_["eb"], jb_["has_next"]
                        grp, slot = gi // 4, gi % 4
                        bk = bank_of(pi, hh, grp)
                        pob = bk[0]
                        vb_ = Vb[pi][:, gi, hh, :]
                        rk = [("PT", eb), ("Vb", pi, gi // 8)]
                        if has_next and slot < 3:
                            first = not bk[1]
                            bk[1] = True
                            S.add("pe", lambda e: e.matmul(ps[pob][0:65, slot * 128:(slot + 2) * 128], lhsT=vb_, rhs=PT[eb][:, 0:256],
                                                           start=first, stop=False, skip_group_check=True),
                                  reads=rk, writes=[("ps", pob)])
                            return
                        first = not bk[1]
                        bk[1] = True
                        S.add("pe", lambda e: e.matmul(ps[pob][0:65, slot * 128:(slot + 1) * 128], lhsT=vb_, rhs=PT[eb][:, 0:128],
                                                       start=first, stop=True, skip_group_check=True),
                              reads=rk, writes=[("ps", pob)])
                        if slot == 3:
                            evac(pi, hh, grp, pob)
                        if has_next:
                            bk2 = bank_of(pi, hh, grp + 1)
                            pob2 = bk2[0]
                            first2 = not bk2[1]
                            bk2[1] = True
                            S.add("pe", lambda e: e.matmul(ps[pob2][0:65, 0:128], lhsT=vb_, rhs=PT[eb][:, 128:256],
                                                           start=first2, stop=False, skip_group_check=True),
                                  reads=rk, writes=[("ps", pob2)])

                    LA = 3
                    for i_ in range(len(jobs) + LA):
                        if i_ < len(jobs):
                            stage_a(jobs[i_])
                        if i_ >= LA:
                            stage_b(jobs[i_ - LA])
                        if i_ % 8 == 0:
                            S.flush_group(gq)
                    if _stg < 6:
                        continue
                    rrow = (64, 32)
                    for hh in range(2):
                        aks = [("acc", hh, g_) for g_ in range(8)]
                        pr = rrow[hh]
                        S.add("act", lambda e, hh=hh, pr=pr: e.activation(out=rdt[pr:pr + 1, :], in_=acc[hh][64:65, :], func=AF.Ln),
                              reads=aks, writes=[("rdt", hh)])
                        S.add("act", lambda e, pr=pr: e.activation(out=rdt[pr:pr + 1, :], in_=rdt[pr:pr + 1, :], func=AF.Exp, scale=-1.0),
                              reads=[("rdt", hh)], writes=[("rdt", hh)])
                    for hh in range(2):
                        pr = rrow[hh]
                        for grp in range(8):
                            gsl = slice(grp * 512, (grp + 1) * 512)
                            nb_ = rc["n"] % 2
                            rc["n"] += 1
                            bk = 6 + nb_
                            mm_group(ps[bk][0:64, :], [(onesf[pr:pr + 1, 0:64], rdt[pr:pr + 1, gsl])], [("rdt", hh)], [("ps", bk)])
                            S.add("dve", lambda e, hh=hh, gsl=gsl, nb_=nb_, bk=bk: e.tensor_tensor(
                                out=onb[nb_][0:64, :], in0=acc[hh][0:64, gsl], in1=ps[bk][0:64, :], op=ALU.mult),
                                reads=[("ps", bk), ("acc", hh, grp)], writes=[("onb", nb_)])
                            row0 = (2 * j + hh) * 64
                            S.dma("sp", mixT_d[row0:row0 + 64, gsl], onb[nb_][0:64, :], reads=[("onb", nb_)],
                                  writes=[("mixT_d", grp)])
        if 2 in phases:
            S.flush_all(gq)
        S.barrier()

        with contextlib.ExitStack() as st3:
            sb3 = lambda n, sh, dt=F32: st3.enter_context(nc.sbuf_tensor("sb_" + n, sh, dt))
            if 3 in phases:
                xin[0] = sb3("xin0c", [128, 8, 512], BF16)
                xin[1] = sb3("xin1c", [128, 8, 512], BF16)
                if 2 not in phases:
                    gates_prologue()
                Wm = [[sb3("Wm%d_%d" % (i, w), [128, 8, 256 if w == 2 else 128], BF16) for w in range(3)] for i in range(2)]
                xc = [sb3("xc%d" % i, [128, SEQ + 3]) for i in range(2)]
                cacc = [sb3("cacc%d" % i, [128, SEQ]) for i in range(2)]
                qkm = [sb3("qTm", [128, SEQ], BF16), sb3("kTm", [128, SEQ], BF16)]
                ktok = sb3("ktok", [128, 32, 128], BF16)
                vpp = sb3("vpp", [128, 32, 129], BF16)
                vpq = sb3("vpq", [128, 32, 129], BF16)
                og = sb3("og", [128, 32, 128])
                sg = [sb3("sg%d" % i, [128, 128]) for i in range(2)]
                C32 = [sb3("C32_%d" % i, [128, 129]) for i in range(2)]
                Cb = [sb3("Cb%d" % i, [128, 129], BF16) for i in range(2)]
                PTm = [sb3("PTm%d" % i, [128, 128], BF16) for i in range(2)]
                om = [sb3("om%d" % i, [128, 128], BF16) for i in range(2)]
                omT = [sb3("omT%d" % i, [128, 512], BF16) for i in range(2)]
                stm = sb3("stm", [128, 32, 8])
                Ocp = [sb3("Ocp%d" % i, [128, 129]) for i in range(4)]
                S.add("pool", lambda e: e.memset(xc[0][:, 0:3], 0.0), writes=[("xcpad", 0)])
                S.add("pool", lambda e: e.memset(xc[1][:, 0:3], 0.0), writes=[("xcpad", 1)])
                inv_sqrt_dh = 1.0 / math.sqrt(128.0)

                def mlstm_head(h):
                    hb = h % 2
                    for w in range(4):
                        c0 = 1536 + w * 512 + h * 128
                        if w < 2:
                            S.dma("pool", Wm[hb][w][:], w_in_v[:, :, c0:c0 + 128], writes=[("Wm", hb, w)])
                        else:
                            S.dma("pool", Wm[hb][2][:, :, (w - 2) * 128:(w - 1) * 128], w_in_v[:, :, c0:c0 + 128], writes=[("Wm", hb, w)])

                    def conv_piece(w, pc):
                        ci = w * 4 + h
                        lo_, hi_ = pc * 1024, (pc + 1) * 1024
                        rk = [("xc", w, t_) for t_ in range(max(0, 2 * pc - 1), 2 * pc + 2)] + [("xcpad", w), "const"]
                        S.add("dve", lambda e: e.tensor_scalar(
                            out=cacc[w][:, lo_:hi_], in0=xc[w][:, lo_ + 3:hi_ + 3], scalar1=cw[:, ci * 4 + 3:ci * 4 + 4],
                            scalar2=None, op0=ALU.mult), reads=rk, writes=[("cacc", w, pc)])
                        for jt in (2, 1, 0):
                            S.add("dve", lambda e, jt=jt: e.scalar_tensor_tensor(
                                out=cacc[w][:, lo_:hi_], in0=xc[w][:, lo_ + jt:hi_ + jt], scalar=cw[:, ci * 4 + jt:ci * 4 + jt + 1],
                                in1=cacc[w][:, lo_:hi_], op0=ALU.mult, op1=ALU.add), reads=rk + [("cacc", w, pc)], writes=[("cacc", w, pc)])
                        S.add("act", lambda e: e.activation(
                            out=qkm[w][:, lo_:hi_], in_=cacc[w][:, lo_:hi_], func=AF.Silu, bias=cb[:, ci:ci + 1]),
                            reads=[("cacc", w, pc), "const"], writes=[("qkm", w, pc)])

                    def vo_sub(xb, tt, sub):
                        c_ = tt * 4 + sub
                        bk = 2 + (c_ % 2)
                        hc = h * 32 + c_

                        def vo(e):
                            ins = None
                            for kc in range(8):
                                ins = e.matmul(ps[bk][:, 0:256], lhsT=xin[xb][:, kc, sub * 128:(sub + 1) * 128],
                                               rhs=Wm[hb][2][:, kc, :], start=(kc == 0), stop=(kc == 7))
                            return ins
                        S.add("pe", vo, reads=[("xin", xb), ("Wm", hb, 2), ("Wm", hb, 3)], writes=[("ps", bk)])
                        S.add("act", lambda e: e.activation(
                            out=vpp[:, c_, 0:128], in_=ps[bk][:, 0:128], func=AF.Copy, scale=etok[:, 0, hc:hc + 1]),
                            reads=[("ps", bk), "etok"], writes=[("vpp", c_)])
                        S.add("act", lambda e: e.activation(
                            out=vpq[:, c_, 0:128], in_=ps[bk][:, 0:128], func=AF.Copy, scale=etok[:, 1, hc:hc + 1]),
                            reads=[("ps", bk), "etok"], writes=[("vpq", c_)])
                        S.add("act", lambda e: e.activation(out=og[:, c_, :], in_=ps[bk][:, 128:256], func=AF.Copy),
                              reads=[("ps", bk)], writes=[("og", c_)])
                        S.add("pool", lambda e: e.tensor_copy(out=vpp[:, c_, 128:129], in_=etok[:, 0, hc:hc + 1]),
                              reads=["etok", ("vpp", c_)], writes=[("vpp", c_)])
                        S.add("pool", lambda e: e.tensor_copy(out=vpq[:, c_, 128:129], in_=etok[:, 1, hc:hc + 1]),
                              reads=["etok", ("vpq", c_)], writes=[("vpq", c_)])

                    for tt in range(8):
                        xb = load_xin(tt)
                        for w in range(2):
                            bk = 5 + w
                            mm_group(ps[bk][:, :], [(Wm[hb][w][:, kc, :], xin[xb][:, kc, :]) for kc in range(8)],
                                     [("Wm", hb, w), ("xin", xb)], [("ps", bk)])
                            S.add("act", lambda e, bk=bk, w=w, tt=tt: e.activation(
                                out=xc[w][:, 3 + tt * 512:3 + (tt + 1) * 512], in_=ps[bk][:, :], func=AF.Copy),
                                reads=[("ps", bk)], writes=[("xc", w, tt)])
                        for sub in range(4):
                            vo_sub(xb, tt, sub)
                        if tt % 2 == 1:
                            conv_piece(1, tt // 2)
                            conv_piece(0, tt // 2)
                    for g8 in range(4):
                        ogk = [("og", c_) for c_ in range(g8 * 8, g8 * 8 + 8)]
                        S.add("act", lambda e, g8=g8: e.activation(out=og[:, g8 * 8:(g8 + 1) * 8, :], in_=og[:, g8 * 8:(g8 + 1) * 8, :],
                                                                   func=AF.Sigmoid), reads=ogk, writes=ogk)
                        S.add("pool", lambda e, g8=g8: e.tensor_tensor(
                            out=og[:, g8 * 8:(g8 + 1) * 8, :], in0=og[:, g8 * 8:(g8 + 1) * 8, :],
                            in1=gm[:, h * 128:(h + 1) * 128].unsqueeze(1).to_broadcast([128, 8, 128]), op=ALU.mult),
                            reads=ogk + ["const"], writes=ogk)
                    for g8 in range(4):
                        kbk_ = 7 if g8 % 2 == 0 else 4

                        def trk(e, g8=g8, kbk_=kbk_):
                            ins = None
                            for q in range(8):
                                c_ = g8 * 8 + q
                                ins = e.transpose(out=psb[kbk_][:, q * 128:(q + 1) * 128], in_=qkm[1][:, c_ * 128:(c_ + 1) * 128],
                                                  identity=identb[:])
                            return ins
                        S.add("pe", trk, reads=[("qkm", 1, g8), "const"], writes=[("ps", kbk_)])
                        S.add("act", lambda e, g8=g8, kbk_=kbk_: e.activation(
                            out=ktok[:, g8 * 8:(g8 + 1) * 8, :], in_=psb[kbk_].rearrange("p (q c) -> p q c", q=8),
                            func=AF.Copy, scale=inv_sqrt_dh), reads=[("ps", kbk_)], writes=[("ktok", g8)])

                    def st_mm(c_):
                        csl = slice(c_ * 128, (c_ + 1) * 128)
                        pi_ = c_ % 2
                        mm_group(ps[pi_][:, 0:128], [(qkm[1][:, csl], qkm[0][:, csl])],
                                 [("qkm", 0, c_ // 8), ("qkm", 1, c_ // 8)], [("ps", pi_)])
                        S.add("dve", lambda e: e.tensor_tensor(
                            out=PTm[pi_][:], in0=ps[pi_][:, 0:128], in1=mmask[:], op=ALU.mult),
                            reads=[("ps", pi_), "const"], writes=[("PTm", pi_)])

                    def core(c_):
                        csl = slice(c_ * 128, (c_ + 1) * 128)
                        hc = h * 32 + c_
                        pi_ = c_ % 2
                        obk = 2 + c_ % 2
                        ubk = 6 if c_ % 2 == 0 else 4
                        cur, nxt = c_ % 2, (c_ + 1) % 2
                        if c_ + 1 < 32:
                            st_mm(c_ + 1)
                        S.add("pe", lambda e: e.matmul(ps[obk][:, 0:129], lhsT=PTm[pi_][:], rhs=vpp[:, c_, :], start=True, stop=(c_ == 0)),
                              reads=[("PTm", pi_), ("vpp", c_)], writes=[("ps", obk)])
                        if c_ < 31:
                            mm_group(ps[ubk][:, 0:129], [(ktok[:, c_, :], vpq[:, c_, :])], [("ktok", c_ // 8), ("vpq", c_)], [("ps", ubk)])
                        if c_ > 0:
                            S.add("pe", lambda e: e.matmul(ps[obk][:, 0:129], lhsT=qkm[0][:, csl], rhs=Cb[cur][:], start=False, stop=True),
                                  reads=[("qkm", 0, c_ // 8), ("Cb", cur)], writes=[("ps", obk)])
                        if c_ < 31:
                            if c_ == 0:
                                S.add("dve", lambda e: e.tensor_copy(out=Cb[nxt][:], in_=ps[ubk][:, 0:129]),
                                      reads=[("ps", ubk)], writes=[("Cb", nxt)])
                                S.add("act", lambda e: e.activation(out=C32[nxt][:], in_=ps[ubk][:, 0:129], func=AF.Copy),
                                      reads=[("ps", ubk)], writes=[("C32", nxt)])
                            else:
                                S.add("dve", lambda e: e.scalar_tensor_tensor(
                                    out=Cb[nxt][:], in0=C32[cur][:], scalar=dec_bc[:, hc:hc + 1], in1=ps[ubk][:, 0:129],
                                    op0=ALU.mult, op1=ALU.add), reads=[("ps", ubk), ("C32", cur), "dec_bc"], writes=[("Cb", nxt)])
                                S.add("dve", lambda e: e.scalar_tensor_tensor(
                                    out=C32[nxt][:], in0=C32[cur][:], scalar=dec_bc[:, hc:hc + 1], in1=ps[ubk][:, 0:129],
                                    op0=ALU.mult, op1=ALU.add), reads=[("ps", ubk), ("C32", cur), "dec_bc"], writes=[("C32", nxt)])

                    def out_a1(c_):
                        obk = 2 + c_ % 2
                        oc = Ocp[c_ % 4]
                        S.add("act", lambda e: e.activation(out=oc[:], in_=ps[obk][:, 0:129], func=AF.Copy),
                              reads=[("ps", obk)], writes=[("Ocp", c_ % 4)])
                        S.add("act", lambda e: e.activation(
                            out=stm[:, c_, 6:7], in_=oc[:, 128:129], func=AF.Abs), reads=[("Ocp", c_ % 4)], writes=[(("stm", c_), 6)])

                    def out_d1(c_):
                        hc = h * 32 + c_
                        sk_ = ("stm", c_)
                        S.add("dve", lambda e: e.tensor_tensor(
                            out=stm[:, c_, 0:1], in0=stm[:, c_, 6:7], in1=etok[:, 2, hc:hc + 1], op=ALU.max),
                            reads=[(sk_, 6), "etok"], writes=[(sk_, 0)])
                        S.add("dve", lambda e: e.reciprocal(out=stm[:, c_, 1:2], in_=stm[:, c_, 0:1]),
                              reads=[(sk_, 0)], writes=[(sk_, 1)])

                    def out_a2(c_):
                        oc = Ocp[c_ % 4]
                        sk_ = ("stm", c_)

                        S.add("act", lambda e: e.activation(out=junk[:, 0:128], in_=oc[:, 0:128], func=AF.Square, scale=stm[:, c_, 1:2],
                                                            accum_out=stm[:, c_, 2:3]),
                              reads=[("Ocp", c_ % 4), (sk_, 1)], writes=[(sk_, 20)])
                        S.add("act", lambda e: e.activation(out=stm[:, c_, 7:8], in_=stm[:, c_, 2:3], func=AF.Copy),
                              reads=[(sk_, 20)], writes=[(sk_, 2)])
                        S.add("act", lambda e: e.activation(out=stm[:, c_, 3:4], in_=stm[:, c_, 2:3], func=AF.Sqrt,
                                                            scale=1.0 / 128, bias=EPS), reads=[(sk_, 2)], writes=[(sk_, 3)])

                    def out_d2(c_):
                        oc = Ocp[c_ % 4]
                        pi_ = c_ % 2
                        sk_ = ("stm", c_)
                        S.add("dve", lambda e: e.reciprocal(out=stm[:, c_, 4:5], in_=stm[:, c_, 3:4]),
                              reads=[(sk_, 3)], writes=[(sk_, 4)])
                        S.add("dve", lambda e: e.tensor_tensor(out=stm[:, c_, 5:6], in0=stm[:, c_, 4:5], in1=stm[:, c_, 1:2], op=ALU.mult),
                              reads=[(sk_, 4), (sk_, 1)], writes=[(sk_, 5)])
                        S.add("dve", lambda e: e.scalar_tensor_tensor(
                            out=om[pi_][:], in0=oc[:, 0:128], scalar=stm[:, c_, 5:6], in1=og[:, c_, :],
                            op0=ALU.mult, op1=ALU.mult), reads=[("Ocp", c_ % 4), (sk_, 5), ("og", c_)], writes=[("om", pi_)])

                    def out_t(c_):
                        pi_ = c_ % 2
                        q4 = c_ % 4
                        tb_ = (c_ // 4) % 2
                        S.add("pe", lambda e: e.transpose(out=psb[7][:, q4 * 128:(q4 + 1) * 128], in_=om[pi_][:], identity=identb[:]),
                              reads=[("om", pi_), "const"], writes=[("ps", 7)])
                        if q4 == 3:
                            S.add("act", lambda e: e.activation(out=omT[tb_][:], in_=psb[7][:, 0:512], func=AF.Copy),
                                  reads=[("ps", 7)], writes=[("omT", tb_)])
                            t0 = (c_ // 4) * 512
                            S.dma("sp", mixT_d[512 + h * 128:512 + (h + 1) * 128, t0:t0 + 512], omT[tb_][:],
                                  reads=[("omT", tb_)], writes=[("mixT_d", c_ // 4)])

                    st_mm(0)
                    for s_ in range(32 + 4):
                        if s_ < 32:
                            core(s_)
                        for fn_, lag in ((out_a1, 1), (out_d1, 2), (out_a2, 2), (out_d2, 3), (out_t, 4)):
                            if 0 <= s_ - lag < 32:
                                fn_(s_ - lag)

                for h in range(4):
                    mlstm_head(h)
        S.barrier()

        stg.close()
        with contextlib.ExitStack() as st4:
            sb4 = lambda n, sh, dt=F32: st4.enter_context(nc.sbuf_tensor("sb_" + n, sh, dt))
            if 4 in phases:
                Wout = sb4("Wout", [128, 8, DM], BF16)
                Wpg = sb4("Wpg", [128, 8, DM], BF16)
                Wple = sb4("Wple", [128, 2, DM], BF16)
                WU = [sb4("WU%d" % i, [128, 8, 512], BF16) for i in range(3)]
                WD = [sb4("WD%d" % i, [128, 4, 512], BF16) for i in range(3)]
                hid = sb4("hid", [128, 32, 512], BF16)
                rl = [sb4("rl%d" % i, [128, 512], BF16) for i in range(2)]
                xt4s = [sb4("xt4_%d" % i, [128, 4, DM]) for i in range(2)]
                mTs = [sb4("mT%d" % i, [128, 8, 512], BF16) for i in range(2)]
                uT = sb4("uT", [128, 8, 512], BF16)
                ub2 = [sb4("ub2_%d" % i, [128, DM], BF16) for i in range(2)]
                pt4s = [sb4("pt4_%d" % i, [128, 4, 256]) for i in range(2)]
                pb = sb4("pb", [128, 4, 256], BF16)
                pTt = sb4("pTt", [128, 2, 512], BF16)
                gt = [sb4("gt%d" % i, [128, 512]) for i in range(2)]
                tm = [sb4("tm%d" % i, [128, 512]) for i in range(2)]
                st2t = sb4("st2t", [128, 8 * 4 * 3 * 3])
                S.dma("pool", Wout[:], w_out.rearrange("(kc p) n -> p kc n", p=128), writes=["Wout"])
                S.dma("pool", Wpg[:], w_pg.rearrange("(kc p) n -> p kc n", p=128), writes=["Wpg"])
                S.dma("pool", Wple[:], w_ple.rearrange("(kc p) n -> p kc n", p=128), writes=["Wple"])
                w_up_v = w_up.rearrange("(kc p) n -> p kc n", p=128)
                w_down_v = w_down.rearrange("(fc p) n -> p fc n", p=128)
                wc = {"u": 0, "d": 0, "b": 0}

                def norm_to_uT(stt, which, gcol, xt4, XK):
                    for s_ in range(4):
                        base = ((stt * 4 + s_) * 3 + which) * 3
                        rms_rstd(xt4[:, s_, :], base, base + 1, base + 2, st2t, [XK(s_)], ("n2", stt, s_, which))
                    for s_ in range(4):
                        base = ((stt * 4 + s_) * 3 + which) * 3
                        tag = ("n2", stt, s_, which)
                        u_ = s_ % 2
                        S.add("act", lambda e, s_=s_, u_=u_, base=base: e.activation(
                            out=ub2[u_][:], in_=xt4[:, s_, :], func=AF.Copy, scale=st2t[:, base + 2:base + 3]),
                            reads=[XK(s_), (tag, "r")], writes=[("ub2", u_)])
                        bk = 2 + u_

                        def tr(e, u_=u_, bk=bk):
                            ins = None
                            for kc in range(8):
                                ins = e.transpose(out=psb[bk][:, kc * 128:(kc + 1) * 128],
                                                  in_=ub2[u_][:, kc * 128:(kc + 1) * 128], identity=identb[:])
                            return ins
                        S.add("pe", tr, reads=[("ub2", u_), "const"], writes=[("ps", bk)])
                        S.add("dve", lambda e, s_=s_, bk=bk, gcol=gcol: e.tensor_tensor(
                            out=uT[:, :, s_ * 128:(s_ + 1) * 128], in0=psb[bk].rearrange("p (c t) -> p c t", c=8),
                            in1=gT[:, gcol:gcol + 8].unsqueeze(2).to_broadcast([128, 8, 128]), op=ALU.mult),
                            reads=[("ps", bk), "const"], writes=[("uT", s_)])

                def load_st(stt):
                    b = stt % 2
                    t0 = stt * 512
                    S.dma("sp", xt4s[b][:], x[t0:t0 + 512, :].rearrange("(s p) d -> p s d", p=128),
                          writes=[("xt4", b, s_) for s_ in range(4)])
                    S.dma("sp", mTs[b][:], mixT_v[:, :, t0:t0 + 512], reads=[("mixT_d", stt)], writes=[("mT", b)])
                    S.dma("sp", pt4s[b][:], pin[t0:t0 + 512, :].rearrange("(s p) d -> p s d", p=128), writes=[("pt4", b)])

                def supertile(stt):
                    b = stt % 2
                    t0 = stt * 512
                    xt4, mT, pt4 = xt4s[b], mTs[b], pt4s[b]
                    XK = lambda s_: ("xt4", b, s_)
                    for s_ in range(4):
                        for hf in range(2):
                            bk = wc["b"] % 2
                            wc["b"] += 1
                            mm_group(ps[bk][:, :], [(mT[:, kc, s_ * 128:(s_ + 1) * 128], Wout[:, kc, hf * 512:(hf + 1) * 512]) for kc in range(8)],
                                     [("mT", b), "Wout"], [("ps", bk)])
                            S.add("dve", lambda e, s_=s_, hf=hf, bk=bk: e.tensor_tensor(
                                out=xt4[:, s_, hf * 512:(hf + 1) * 512], in0=ps[bk][:, :], in1=xt4[:, s_, hf * 512:(hf + 1) * 512], op=ALU.add),
                                reads=[("ps", bk), XK(s_)], writes=[XK(s_)])
                    if stt + 1 < 8:
                        load_st(stt + 1)
                    norm_to_uT(stt, 0, 8, xt4, XK)
                    for g in range(8):
                        ws = wc["u"] % 3
                        wc["u"] += 1
                        S.dma("pool", WU[ws][:], w_up_v[:, :, g * 512:(g + 1) * 512], writes=[("WU", ws)])
                        for q in range(4):
                            fc = g * 4 + q
                            bk = wc["b"] % 2
                            wc["b"] += 1
                            mm_group(ps[bk][:, :], [(WU[ws][:, kc, q * 128:(q + 1) * 128], uT[:, kc, :]) for kc in range(8)],
                                     [("WU", ws)] + [("uT", s_) for s_ in range(4)], [("ps", bk)])
                            S.add("act", lambda e, bk=bk: e.activation(out=rl[bk][:], in_=ps[bk][:, :], func=AF.Relu),
                                  reads=[("ps", bk)], writes=[("rl", bk)])
                            S.add("dve", lambda e, bk=bk, fc=fc: e.tensor_tensor(out=hid[:, fc, :], in0=rl[bk][:], in1=rl[bk][:], op=ALU.mult),
                                  reads=[("rl", bk)], writes=[("hid", fc)])
                    for hf in range(2):
                        for g in range(8):
                            ws = wc["d"] % 3
                            wc["d"] += 1
                            S.dma("pool", WD[ws][:], w_down_v[:, g * 4:(g + 1) * 4, hf * 512:(hf + 1) * 512], writes=[("WD", ws)])
                            for s_ in range(4):
                                def dn(e, ws=ws, s_=s_, g=g):
                                    ins = None
                                    for q in range(4):
                                        ins = e.matmul(ps[4 + s_][:, :], lhsT=hid[:, g * 4 + q, s_ * 128:(s_ + 1) * 128], rhs=WD[ws][:, q, :],
                                                       start=(g == 0 and q == 0), stop=(g == 7 and q == 3))
                                    return ins
                                S.add("pe", dn, reads=[("WD", ws)] + [("hid", g * 4 + q) for q in range(4)], writes=[("ps", 4 + s_)])
                        for s_ in range(4):
                            S.add("dve", lambda e, s_=s_, hf=hf: e.tensor_tensor(
                                out=xt4[:, s_, hf * 512:(hf + 1) * 512], in0=ps[4 + s_][:, :], in1=xt4[:, s_, hf * 512:(hf + 1) * 512], op=ALU.add),
                                reads=[("ps", 4 + s_), XK(s_)], writes=[XK(s_)])
                    norm_to_uT(stt, 1, 16, xt4, XK)
                    S.add("act", lambda e: e.activation(out=pb[:].rearrange("p s d -> p (s d)"), in_=pt4[:].rearrange("p s d -> p (s d)"), func=AF.Copy),
                          reads=[("pt4", b)], writes=["pb"])

                    def trp(e):
                        ins = None
                        for s_ in range(4):
                            for pc in range(2):
                                ins = e.transpose(out=psb[2][:, pc * 512 + s_ * 128:pc * 512 + (s_ + 1) * 128],
                                                  in_=pb[:, s_, pc * 128:(pc + 1) * 128], identity=identb[:])
                        return ins
                    S.add("pe", trp, reads=["pb", "const"], writes=[("ps", 2)])
                    S.add("act", lambda e: e.activation(out=pTt[:].rearrange("p a b -> p (a b)"), in_=psb[2][:, :], func=AF.Copy),
                          reads=[("ps", 2)], writes=["pTt"])
                    for s_ in range(4):
                        for hf in range(2):
                            bk = wc["b"] % 2
                            wc["b"] += 1
                            hsl = slice(hf * 512, (hf + 1) * 512)
                            mm_group(ps[bk][:, :], [(uT[:, kc, s_ * 128:(s_ + 1) * 128], Wpg[:, kc, hsl]) for kc in range(8)],
                                     [("uT", s_), "Wpg"], [("ps", bk)])
                            S.add("act", lambda e, bk=bk: e.activation(out=gt[bk][:], in_=ps[bk][:, :], func=AF.Sigmoid),
                                  reads=[("ps", bk)], writes=[("gt", bk)])
                            mm_group(ps[2 + bk][:, :], [(pTt[:, pc, s_ * 128:(s_ + 1) * 128], Wple[:, pc, hsl]) for pc in range(2)],
                                     ["pTt", "Wple"], [("ps", 2 + bk)])
                            S.add("dve", lambda e, bk=bk: e.tensor_tensor(out=tm[bk][:], in0=ps[2 + bk][:, :], in1=gt[bk][:], op=ALU.mult),
                                  reads=[("ps", 2 + bk), ("gt", bk)], writes=[("tm", bk)])
                            S.add("dve", lambda e, bk=bk, s_=s_, hsl=hsl: e.tensor_tensor(
                                out=xt4[:, s_, hsl], in0=tm[bk][:], in1=xt4[:, s_, hsl], op=ALU.add),
                                reads=[("tm", bk), XK(s_)], writes=[XK(s_)])
                    for s_ in range(4):
                        base = ((stt * 4 + s_) * 3 + 2) * 3
                        tag = ("n2", stt, s_, 2)
                        rms_rstd(xt4[:, s_, :], base, base + 1, base + 2, st2t, [XK(s_)], tag)
                        S.add("dve", lambda e, s_=s_, base=base: e.scalar_tensor_tensor(
                            out=xt4[:, s_, :], in0=xt4[:, s_, :], scalar=st2t[:, base + 2:base + 3], in1=gfin[:],
                            op0=ALU.mult, op1=ALU.mult), reads=[XK(s_), (tag, "r"), "const"], writes=[XK(s_)])
                        r0 = t0 + s_ * 128
                        S.dma("sp", out[r0:r0 + 128, :], xt4[:, s_, :], reads=[XK(s_)], writes=[("out", stt, s_)])
                load_st(0)
                for stt in range(8):
                    supertile(stt)
        S.emit()
    return nc


def _consts():
    bf = ml_dtypes.bfloat16
    c = {}
    c["identb"] = np.eye(128, dtype=np.float32).astype(bf)
    c["identf"] = np.eye(128, dtype=np.float32)
    pidx = np.arange(128)[:, None]
    fidx = np.arange(128)[None, :]
    prev = (pidx >= fidx).astype(np.float32)
    cur = (pidx <= fidx).astype(np.float32)
    c["amask"] = np.concatenate([prev, cur], axis=1).astype(bf)
    c["amask2"] = np.concatenate([cur, prev], axis=1).astype(bf)
    c["mmask"] = (cur / math.sqrt(128.0)).astype(np.float32)
    rm = np.zeros((128, 128), np.float32)
    cosT = np.ones((128, SEQ), np.float32)
    sinT = np.zeros((128, SEQ), np.float32)
    half = 8
    inv_freq = np.power(np.float32(500000.0), -np.arange(half, dtype=np.float32) / np.float32(half)).astype(np.float32)
    pos = np.arange(SEQ, dtype=np.float32)
    ang = (pos[None, :] * inv_freq[:, None]).astype(np.float32)
    for hh in range(2):
        for i in range(half):
            m1 = hh * 64 + i
            m2 = hh * 64 + i + half
            rm[m2, m1] = 1.0
            rm[m1, m2] = 1.0
            cosT[m1] = np.cos(ang[i]); cosT[m2] = np.cos(ang[i])
            sinT[m1] = -np.sin(ang[i]); sinT[m2] = np.sin(ang[i])
    c["rm"] = rm.astype(bf)
    c["ropec"] = cosT
    c["ropes"] = sinT
    lm = np.zeros((128, 128), np.float32)
    for h in range(4):
        for c1 in range(32):
            for c2 in range(c1 + 1, 32):
                lm[h * 32 + c1, h * 32 + c2] = 1.0
    c["lmat"] = lm
    return c


_NC_CACHE = {}


def _prep_shared(inp):
    f = lambda a: np.ascontiguousarray(np.asarray(a, dtype=np.float32))
    sh = {}
    sh["w_in"] = f(inp["w_in"][0])
    sh["w_out"] = f(inp["w_out"][0])
    sh["w_up"] = f(inp["w_up"][0])
    sh["w_down"] = f(inp["w_down"][0])
    sh["w_pg"] = f(inp["w_ple_gate"][0])
    sh["w_ple"] = f(inp["w_ple"][0])
    g1 = f(inp["norm_mix_g"][0]).reshape(8, 128).T
    g2 = f(inp["norm_mlp_g"][0]).reshape(8, 128).T
    g3 = f(inp["norm_ple_g"][0]).reshape(8, 128).T
    sh["gT"] = np.ascontiguousarray(np.concatenate([g1, g2, g3], axis=1))
    sh["gfin"] = np.ascontiguousarray(np.broadcast_to(f(inp["final_norm_g"])[None, :], (128, DM)))
    sh["gm"] = np.ascontiguousarray(np.broadcast_to(f(inp["mlstm_norm_g"][0])[None, :], (128, 512)))
    cwv = f(inp["conv_w"][0])
    sh["cw"] = np.ascontiguousarray(cwv.reshape(4, 8, 128).transpose(2, 1, 0).reshape(128, 32))
    sh["cb"] = np.ascontiguousarray(f(inp["conv_b"][0]).reshape(8, 128).T)
    gbv = f(inp["gate_b"][0])
    sh["gb"] = np.ascontiguousarray(np.stack([np.repeat(gbv[0:4], 32), np.repeat(gbv[4:8], 32)], axis=1))
    sh.update(_consts())
    return sh


def kernel(**inputs):
    if "nc" not in _NC_CACHE:
        _NC_CACHE["nc"] = build_nc()
    nc = _NC_CACHE["nc"]
    sh = _prep_shared(inputs)
    x = np.asarray(inputs["x"], dtype=np.float32)
    p = np.asarray(inputs["p"], dtype=np.float32)
    in_maps = []
    for b in range(8):
        m = dict(sh)
        m["x"] = np.ascontiguousarray(x[b])
        m["p"] = np.ascontiguousarray(p[0, b])
        in_maps.append(m)
    res = run_bass_kernel_spmd(nc, in_maps, core_ids=list(range(8)))
    return np.stack([np.asarray(r["out"], dtype=np.float32) for r in res.results], axis=0)
```
